# Optimizing a Trainium2 kernel written in Bass

```python
import math
import jax, jax.numpy as jnp
from jax import lax
import numpy as np

D_MODEL = 1024
BATCH = 16
SEQ = 256
DEPTH = 1
DEC_BATCH = 4
DEC_SEQ = 4096
PAST_LEN = 512

GRID_W = 64
N_ATT_HEADS = 8
DIFF_HEAD_DIM = 64
V_HEAD_DIM = 2 * DIFF_HEAD_DIM
QK_WIDTH = N_ATT_HEADS * 2 * DIFF_HEAD_DIM
D_ATT = N_ATT_HEADS * V_HEAD_DIM
D_LRU = 1024
N_LRU_BLOCKS = 8
LRU_BLOCK = D_LRU // N_LRU_BLOCKS
CONV_WIDTH = 4
LRU_C = 8.0
D_MIX = D_ATT + D_LRU
SPLITS = (QK_WIDTH, 2 * QK_WIDTH, 2 * QK_WIDTH + D_ATT, 2 * QK_WIDTH + 2 * D_ATT, 2 * QK_WIDTH + 2 * D_ATT + D_LRU)
D_IN_TOTAL = 2 * QK_WIDTH + 2 * D_ATT + 2 * D_LRU
ROPE_BASE = 10000.0
Q_BLOCK = 128
EPS = 1e-6

kernel_name = "hybrid_diffattn_rglru_prefix_step"


def rms_norm(x, g):
    xf = x.astype(jnp.float32)
    y = xf * lax.rsqrt(jnp.mean(xf * xf, axis=-1, keepdims=True) + EPS)
    return (y * g.astype(jnp.float32)).astype(x.dtype)


def modulation(cond, w_ada, b_ada):
    m = jax.nn.silu(cond) @ w_ada + b_ada
    shift, scale, gate = jnp.split(m, 3, axis=-1)
    return shift, scale, gate


def axial_rope(t):
    T = t.shape[1]
    rows = T // GRID_W
    row = jnp.repeat(jnp.arange(rows, dtype=jnp.float32), GRID_W)
    col = jnp.tile(jnp.arange(GRID_W, dtype=jnp.float32), rows)
    half = DIFF_HEAD_DIM // 2
    nf = half // 2
    inv = ROPE_BASE ** (-jnp.arange(nf, dtype=jnp.float32) * 2.0 / half)

    def rot(x, pos):
        ang = pos[:, None] * inv[None, :]
        cos = jnp.cos(ang)[None, :, None, None, :]
        sin = jnp.sin(ang)[None, :, None, None, :]
        xf = x.astype(jnp.float32)
        x1, x2 = xf[..., :nf], xf[..., nf:]
        return jnp.concatenate([x1 * cos - x2 * sin, x1 * sin + x2 * cos], axis=-1).astype(x.dtype)

    return jnp.concatenate([rot(t[..., :half], row), rot(t[..., half:], col)], axis=-1)


def diff_attention(q, k, v, lam):
    B, S, H, _, d = q.shape
    nb = S // Q_BLOCK
    scale = 1.0 / math.sqrt(d)
    qb = q.reshape(B, nb, Q_BLOCK, H, 2, d).transpose(1, 0, 2, 3, 4, 5)

    def one_block(qblk):
        s = jnp.einsum('bqhmd,bkhmd->bhmqk', qblk, k).astype(jnp.float32) * scale
        p = jax.nn.softmax(s, axis=-1)
        a = p[:, :, 0] - lam * p[:, :, 1]
        return jnp.einsum('bhqk,bkhe->bqhe', a.astype(v.dtype), v)

    o = lax.map(one_block, qb)
    return o.transpose(1, 0, 2, 3, 4).reshape(B, S, H, v.shape[-1])


def centred_conv(x, w, b):
    T = x.shape[1]
    left = (CONV_WIDTH - 1) // 2
    right = CONV_WIDTH - 1 - left
    xp = jnp.pad(x, ((0, 0), (left, right), (0, 0)))
    y = b
    for j in range(CONV_WIDTH):
        y = y + xp[:, j:j + T] * w[j]
    return y


def lru_coeffs(x, w_r, b_r, w_i, b_i, lam):
    B, T, D = x.shape
    xr = x.reshape(B, T, N_LRU_BLOCKS, LRU_BLOCK)
    r = jax.nn.sigmoid(jnp.einsum('btnc,ncd->btnd', xr, w_r).reshape(B, T, D) + b_r)
    i = jax.nn.sigmoid(jnp.einsum('btnc,ncd->btnd', xr, w_i).reshape(B, T, D) + b_i)
    log_a = -LRU_C * r.astype(jnp.float32) * jax.nn.softplus(-lam.astype(jnp.float32))
    a = jnp.exp(log_a)
    mult = jnp.sqrt(-jnp.expm1(2.0 * log_a))
    return a, mult * (i * x).astype(jnp.float32)


def linear_scan(a, b, h0, reverse):
    def comb(l, r):
        al, bl = l
        ar, br = r
        return al * ar, ar * bl + br
    A, Bc = lax.associative_scan(comb, (a, b), axis=1, reverse=reverse)
    return Bc + A * h0.astype(jnp.float32)[:, None, :]


def sublayer(x, shift, scale, gate, ctx_k, ctx_v, h0_f, h0_b, use_rope, layer,
             g_pre, w_in, lq1, lk1, lq2, lk2, g_subln, conv_w, conv_b,
             w_r, b_r, w_i, b_i, lru_lam, w_out, g_post):
    B, T, _ = x.shape
    h = rms_norm(x, g_pre) * (1.0 + scale) + shift
    p = h @ w_in
    q, k, v, g_att, x_lru, g_lru = jnp.split(p, SPLITS, axis=-1)
    q = q.reshape(B, T, N_ATT_HEADS, 2, DIFF_HEAD_DIM)
    k = k.reshape(B, T, N_ATT_HEADS, 2, DIFF_HEAD_DIM)
    v = v.reshape(B, T, N_ATT_HEADS, V_HEAD_DIM)
    if use_rope:
        q = axial_rope(q)
        k = axial_rope(k)
    if ctx_k is None:
        k_all, v_all = k, v
    else:
        k_all = jnp.concatenate([ctx_k, k], axis=1)
        v_all = jnp.concatenate([ctx_v, v], axis=1)
    lam_init = 0.8 - 0.6 * math.exp(-0.3 * layer)
    lam = (jnp.exp(jnp.sum(lq1.astype(jnp.float32) * lk1.astype(jnp.float32)))
           - jnp.exp(jnp.sum(lq2.astype(jnp.float32) * lk2.astype(jnp.float32))) + lam_init)
    o = diff_attention(q, k_all, v_all, lam)
    att = (rms_norm(o, g_subln) * (1.0 - lam_init)).reshape(B, T, D_ATT) * jax.nn.silu(g_att)
    u = centred_conv(x_lru, conv_w, conv_b)
    a_f, b_f = lru_coeffs(u, w_r[0], b_r[0], w_i[0], b_i[0], lru_lam[0])
    a_b, b_b = lru_coeffs(u, w_r[1], b_r[1], w_i[1], b_i[1], lru_lam[1])
    hf = linear_scan(a_f, b_f, h0_f, False)
    hb = linear_scan(a_b, b_b, h0_b, True)
    lru = (hf + hb).astype(x.dtype) * jax.nn.silu(g_lru)
    out = jnp.concatenate([att, lru], axis=-1) @ w_out
    x_new = x + gate * rms_norm(out, g_post)
    k_flat = k.reshape(B, T, N_ATT_HEADS, 2 * DIFF_HEAD_DIM)
    state = jnp.stack([hf[:, -1], hb[:, 0]], axis=1).astype(x.dtype)
    return x_new, k_flat, v, state


def setup_inputs(seed: int = 0) -> dict:
    key = jax.random.key(seed)
    ks = jax.random.split(key, 32)
    f32 = jnp.float32
    nrm = lambda k, s, sc: jax.random.normal(k, s, f32) * sc
    a0 = jax.random.uniform(ks[20], (DEPTH, 2, D_LRU), f32, 0.9, 0.999)
    s0 = a0 ** (1.0 / LRU_C)
    return {
        "x_prompt": nrm(ks[0], (BATCH, SEQ, D_MODEL), 1.0),
        "x_sample": nrm(ks[1], (DEC_BATCH, DEC_SEQ, D_MODEL), 1.0),
        "cache_k": nrm(ks[2], (DEC_BATCH, DEPTH, PAST_LEN, N_ATT_HEADS, 2 * DIFF_HEAD_DIM), 1.0),
        "cache_v": nrm(ks[3], (DEC_BATCH, DEPTH, PAST_LEN, N_ATT_HEADS, V_HEAD_DIM), 1.0),
        "state_lru": nrm(ks[4], (DEC_BATCH, DEPTH, 2, D_LRU), 0.5),
        "c": nrm(ks[5], (DEC_BATCH, D_MODEL), 1.0),
        "c_ctx": nrm(ks[6], (D_MODEL,), 1.0),
        "w_ada": nrm(ks[7], (DEPTH, D_MODEL, 3 * D_MODEL), 0.5 * D_MODEL ** -0.5),
        "b_ada": nrm(ks[8], (DEPTH, 3 * D_MODEL), 0.02),
        "g_pre": 1.0 + nrm(ks[9], (DEPTH, D_MODEL), 0.02),
        "w_in": nrm(ks[10], (DEPTH, D_MODEL, D_IN_TOTAL), D_MODEL ** -0.5),
        "lambda_q1": nrm(ks[11], (DEPTH, DIFF_HEAD_DIM), 0.1),
        "lambda_k1": nrm(ks[12], (DEPTH, DIFF_HEAD_DIM), 0.1),
        "lambda_q2": nrm(ks[13], (DEPTH, DIFF_HEAD_DIM), 0.1),
        "lambda_k2": nrm(ks[14], (DEPTH, DIFF_HEAD_DIM), 0.1),
        "g_subln": 1.0 + nrm(ks[15], (DEPTH, V_HEAD_DIM), 0.02),
        "conv_w": nrm(ks[16], (DEPTH, CONV_WIDTH, D_LRU), CONV_WIDTH ** -0.5),
        "conv_b": nrm(ks[17], (DEPTH, D_LRU), 0.02),
        "w_rgate": nrm(ks[18], (DEPTH, 2, N_LRU_BLOCKS, LRU_BLOCK, LRU_BLOCK), LRU_BLOCK ** -0.5),
        "b_rgate": nrm(ks[19], (DEPTH, 2, D_LRU), 0.02),
        "w_igate": nrm(ks[21], (DEPTH, 2, N_LRU_BLOCKS, LRU_BLOCK, LRU_BLOCK), LRU_BLOCK ** -0.5),
        "b_igate": nrm(ks[22], (DEPTH, 2, D_LRU), 0.02),
        "lru_lambda": jnp.log(s0) - jnp.log1p(-s0),
        "w_out": nrm(ks[23], (DEPTH, D_MIX, D_MODEL), D_MIX ** -0.5),
        "g_post": 1.0 + nrm(ks[24], (DEPTH, D_MODEL), 0.02),
    }


def reference(x_prompt, x_sample, cache_k, cache_v, state_lru, c, c_ctx, w_ada, b_ada, g_pre, w_in,
              lambda_q1, lambda_k1, lambda_q2, lambda_k2, g_subln, conv_w, conv_b,
              w_rgate, b_rgate, w_igate, b_igate, lru_lambda, w_out, g_post):
    y_p = x_prompt
    y_s = x_sample
    Bp = x_prompt.shape[0]
    Bd, Kc = cache_k.shape[0], cache_k.shape[2]
    zeros = jnp.zeros((Bp, D_LRU), jnp.float32)
    new_ks, new_vs, new_sts = [], [], []
    for l in range(DEPTH):
        w = (g_pre[l], w_in[l], lambda_q1[l], lambda_k1[l], lambda_q2[l], lambda_k2[l], g_subln[l],
             conv_w[l], conv_b[l], w_rgate[l], b_rgate[l], w_igate[l], b_igate[l], lru_lambda[l],
             w_out[l], g_post[l])
        sh, sc, gt = modulation(c_ctx, w_ada[l], b_ada[l])
        y_p, k_l, v_l, st_l = sublayer(y_p, sh, sc, gt, None, None, zeros, zeros, False, l, *w)
        new_ks.append(k_l)
        new_vs.append(v_l)
        new_sts.append(st_l)
        sh, sc, gt = modulation(c[:, None, :], w_ada[l], b_ada[l])
        ctx_k = cache_k[:, l].reshape(Bd, Kc, N_ATT_HEADS, 2, DIFF_HEAD_DIM)
        y_s, _, _, _ = sublayer(y_s, sh, sc, gt, ctx_k, cache_v[:, l], state_lru[:, l, 0], state_lru[:, l, 1],
                                True, l, *w)
    new_k = jnp.stack(new_ks, axis=1)
    new_v = jnp.stack(new_vs, axis=1)
    new_state_lru = jnp.stack(new_sts, axis=1)
    return (y_p, y_s, new_k, new_v, new_state_lru)
```

```python
import contextlib
import numpy as np
import ml_dtypes
import concourse.bass as bass
import concourse.mybir as mybir
from concourse.bass_utils import run_bass_kernel_spmd

F32 = mybir.dt.float32
BF16 = mybir.dt.bfloat16
AF = mybir.ActivationFunctionType
ALU = mybir.AluOpType
AX = mybir.AxisListType

NCORES = 8
T_S = 4096
T_OWN = 2048
T_P = 512
EPS = 1e-6
LAM_INIT = 0.2
ENGINES = ("sync", "scalar", "vector", "gpsimd", "tensor")
STOP_AT = None
import os
PBMOD = int(os.environ.get('PBMOD', '4'))
EXPT = os.environ.get('EXPT', '')


class Op:
    __slots__ = ("eng", "fn", "deps", "idx", "dma_key", "signal", "ordinal", "is_dma")

    def __init__(self, eng, fn, idx, dma_key):
        self.eng = eng
        self.fn = fn
        self.idx = idx
        self.dma_key = dma_key
        self.is_dma = dma_key is not None
        self.deps = []
        self.signal = False
        self.ordinal = 0


class Prog:
    def __init__(self, nc):
        self.nc = nc
        self.ops = []
        self.last_w = {}
        self.readers = {}
        self.fence = []
        self.marks = []
        self.cut = None

    def add(self, eng, fn, reads=(), writes=(), dma_key=None):
        op = Op(eng, fn, len(self.ops), dma_key)
        deps = set(self.fence)
        for r in reads:
            w = self.last_w.get(r)
            if w is not None:
                deps.add(w)
            if eng != "tensor" and isinstance(r, tuple) and r[0] == "ps":
                for rd in self.readers.get(r, ()):
                    if rd.eng != eng:
                        deps.add(rd)
        for r in writes:
            w = self.last_w.get(r)
            if w is not None:
                deps.add(w)
            for rd in self.readers.get(r, ()):
                deps.add(rd)
        for r in reads:
            self.readers.setdefault(r, []).append(op)
        for r in writes:
            self.last_w[r] = op
            self.readers[r] = []
        best = {}
        for d in deps:
            if d.eng == "tensor" and eng == "tensor" and not d.is_dma:
                continue
            k = ("dma", d.dma_key) if d.is_dma else ("eng", d.eng)
            if k not in best or best[k].idx < d.idx:
                best[k] = d
        op.deps = list(best.values())
        self.ops.append(op)
        return op

    def set_fence(self):
        last = {}
        for op in self.ops:
            k = ("dma", op.dma_key) if op.is_dma else ("eng", op.eng)
            last[k] = op
        self.fence = list(last.values())
        self.marks.append(len(self.ops))

    def emit(self):
        nc = self.nc
        if self.cut is not None:
            self.ops = self.ops[:self.cut]
        for op in self.ops:
            for d in op.deps:
                d.signal = True
        dma_keys = []
        seen = set()
        dma_owner = {}
        for op in self.ops:
            if op.is_dma:
                op.signal = True
                dma_owner[op.dma_key] = op.eng
                if op.dma_key not in seen:
                    seen.add(op.dma_key)
                    dma_keys.append(op.dma_key)
        cnt = {}
        for op in self.ops:
            if op.is_dma:
                k = ("dma", op.dma_key)
                cnt[k] = cnt.get(k, 0) + 16
                op.ordinal = cnt[k]
            elif op.signal:
                k = ("eng", op.eng)
                cnt[k] = cnt.get(k, 0) + 1
                op.ordinal = cnt[k]
        with contextlib.ExitStack() as st:
            sems = {}
            for e in ENGINES:
                sems[("eng", e)] = st.enter_context(nc.semaphore("s_" + e))
            for i, k in enumerate(dma_keys):
                sems[("dma", k)] = st.enter_context(nc.semaphore("d%d" % i))
            block = st.enter_context(nc.Block())
            per_eng = {e: [] for e in ENGINES}
            for op in self.ops:
                per_eng[op.eng].append(op)

            def body(ename):
                def run(eng):
                    waited = {}
                    for op in per_eng[ename]:
                        for d in op.deps:
                            k = ("dma", d.dma_key) if d.is_dma else ("eng", d.eng)
                            if waited.get(k, 0) >= d.ordinal:
                                continue
                            waited[k] = d.ordinal
                            eng.wait_ge(sems[k], d.ordinal)
                        ins = op.fn(eng)
                        if op.signal:
                            k = ("dma", op.dma_key) if op.is_dma else ("eng", op.eng)
                            ins.then_inc(sems[k], 16 if op.is_dma else 1)
                    for k, v in cnt.items():
                        if k[0] == "dma" and dma_owner.get(k[1]) == ename and waited.get(k, 0) < v:
                            eng.wait_ge(sems[k], v)
                return run

            block.sync(body("sync"))
            block.scalar(body("scalar"))
            block.vector(body("vector"))
            block.gpsimd(body("gpsimd"))
            block.tensor(body("tensor"))


class Arena:
    def __init__(self, nc, nbytes):
        self.ap = nc.alloc_sbuf_tensor("arena", [128, nbytes // 4], F32).ap()
        self.off = 0
        self.nbytes = nbytes

    def alloc(self, free_shape, dtype):
        isz = 4 if dtype == F32 else 2
        n = int(np.prod(free_shape))
        nb = (n * isz + 31) // 32 * 32
        assert self.off + nb <= self.nbytes, ("SBUF arena overflow", self.off, nb)
        v = self.ap[:, self.off // 4:(self.off + nb) // 4]
        self.off += nb
        if dtype != F32:
            v = v.bitcast(dtype)
        v = v[:, 0:n]
        if len(free_shape) == 2:
            v = v.rearrange("p (a b) -> p a b", a=free_shape[0])
        elif len(free_shape) == 3:
            v = v.rearrange("p (a b c) -> p a b c", a=free_shape[0], b=free_shape[1])
        return v


def bcast_last(ap2, n):
    return bass.AP(ap2.tensor, ap2.offset, list(ap2.ap) + [[0, n]])


def bcast_mid(ap2, n):
    a = list(ap2.ap)
    return bass.AP(ap2.tensor, ap2.offset, [a[0], [0, n]] + a[1:])


def build_program():
    nc = bass.Bass("TRN2", target_bir_lowering=False)
    P = Prog(nc)
    notes = []

    def note(nm):
        notes.append((nm, len(P.ops)))

    def din(name, shape, dt=F32):
        return nc.dram_tensor(name, list(shape), dt, kind="ExternalInput").ap()

    def dout(name, shape):
        return nc.dram_tensor(name, list(shape), F32, kind="ExternalOutput").ap()

    xs = din("xs", [T_S, 1024])
    xp = din("xp", [T_P, 1024])
    ck = din("ck", [512, 1024])
    cv = din("cv", [512, 1024])
    cs_d = din("cs", [128, 16])
    w_ada = din("w_ada", [1024, 3072])
    bada_ss = din("bada_ss", [128, 16])
    bada_g = din("bada_g", [128, 1024])
    gpre_c = din("gpre_c", [128, 8])
    gpost_r = din("gpost_r", [128, 1024])
    w_att = din("w_att", [8, 1024, 512])
    pmat_d = din("pmat", [128, 128])
    rope_cos = din("rope_cos", [128, T_S])
    rope_ssin = din("rope_ssin", [128, T_S])
    w_lru = din("w_lru", [8, 1024, 256])
    w_gate = din("w_gate", [128, 8 * 4 * 128])
    lru_vec = din("lru_vec", [128, 8 * 14])
    w_out = din("w_out", [2048, 1024])
    lam_v = din("lam_v", [128, 256])
    gsub_r = din("gsub_r", [128, 128])

    y_s = dout("y_s", [T_OWN, 1024])
    y_p = dout("y_p", [T_P, 1024])
    nk = dout("nk", [T_P, 1024])
    nv = dout("nv", [T_P, 1024])
    nst = dout("nst", [128, 32])

    NTOK = T_OWN + T_P
    mix_d = nc.dram_tensor("mix_d", [2048, NTOK], BF16, kind="Internal").ap()
    gs_d = nc.dram_tensor("gs_d", [128, 2048], F32, kind="Internal").ap()

    ar = Arena(nc, 212736)
    psum = nc.alloc_psum_tensor("psum", [128, 8, 512], F32).ap()

    def bank(i):
        return psum[:, i, :]

    def bank_bf(i):
        return psum[:, i, :].bitcast(BF16)

    def mm(out, lhsT, rhs, start, stop, reads, writes, skip=False):
        P.add("tensor", lambda e: e.matmul(out, lhsT=lhsT, rhs=rhs, start=start, stop=stop,
                                           skip_group_check=skip), reads, writes)

    def tr(out, in_, reads, writes):
        P.add("tensor", lambda e: e.transpose(out=out, in_=in_, identity=ident), list(reads) + ["ident"], writes)

    def act(out, in_, func, reads, writes, scale=1.0, bias=0.0, accum=None):
        if accum is None:
            P.add("scalar", lambda e: e.activation(out=out, in_=in_, func=func, scale=scale, bias=bias), reads, writes)
        else:
            P.add("scalar", lambda e: e.activation(out=out, in_=in_, func=func, scale=scale, bias=bias,
                                                   accum_out=accum), reads, writes)

    def veng(eng, name, reads, writes, *args, **kw):
        P.add(eng, lambda e: getattr(e, name)(*args, **kw), reads, writes)

    def dve(name, reads, writes, *args, **kw):
        veng("vector", name, reads, writes, *args, **kw)

    def pool(name, reads, writes, *args, **kw):
        veng("gpsimd", name, reads, writes, *args, **kw)

    def dma(eng, out, in_, reads, writes, key):
        if eng == "gpsimd" and out.dtype != in_.dtype:
            P.add(eng, lambda e: e.dma_start(out=out, in_=in_, max_dma_last_dim=4096), reads, writes, dma_key=key)
        else:
            P.add(eng, lambda e: e.dma_start(out=out, in_=in_), reads, writes, dma_key=key)

    ident = ar.alloc([128], BF16)
    pmat = ar.alloc([128], BF16)
    hT_off = ar.off
    hT = ar.alloc([8, T_S + T_P], BF16)
    modAB = ar.alloc([16, 2], F32)
    nlam = ar.alloc([2], F32)
    gs = ar.alloc([128], F32)
    lvec = ar.alloc([8, 14], F32)
    lcon = ar.alloc([8, 8], F32)
    st_out = ar.alloc([8, 4], F32)
    small = ar.alloc([64], F32)
    zero_c = ar.alloc([1], F32)
    xbase = ar.off
    wA = [ar.alloc([8, 512], BF16) for _ in range(2)]
    cosT = ar.alloc([T_S], F32)
    sinT = ar.alloc([T_S], F32)
    cbase = ar.off

    pool("memset", [], ["ident"], ident, 0.0)
    P.add("gpsimd", lambda e: e.affine_select(out=ident, in_=ident, pattern=[[-1, 128]], compare_op=ALU.not_equal,
                                              fill=1.0, base=0, channel_multiplier=1), ["ident"], ["ident"])
    pool("memset", [], ["zero_c"], zero_c, 0.0)
    watt_v = [w_att[h].rearrange("(kc p) n -> p kc n", p=128) for h in range(8)]

    def load_head_w(h):
        dma("gpsimd", wA[h % 2], watt_v[h], [], [("wA", h % 2)], ("wA", h % 2))

    load_head_w(0)
    dma("gpsimd", pmat, pmat_d, [], ["pmat"], "pmat")

    cs_t = ar.alloc([8, 2], F32)
    Gs = ar.alloc([2, 1024], F32)
    cs_a = ar.alloc([8, 2], F32)
    csb = ar.alloc([8, 2, 128], BF16)
    cs_b = ar.alloc([8, 2], BF16)
    bss = ar.alloc([16], F32)
    gpre_t = ar.alloc([8], F32)
    bg_t = ar.alloc([1024], F32)
    gpo_t = ar.alloc([1024], F32)
    lam_t = ar.alloc([4, 64], F32)
    lam_p = ar.alloc([2, 64], F32)
    wa = [ar.alloc([8, 256], BF16) for _ in range(4)]

    dma("sync", cs_t.rearrange("p a b -> p (a b)"), cs_d, [], ["cs_t"], "cs_t")
    dma("sync", bss, bada_ss, [], ["bss"], "bss")
    dma("sync", gpre_t, gpre_c, [], ["gpre_t"], "gpre_t")
    dma("sync", bg_t, bada_g, [], ["bg_t"], "bg_t")
    dma("sync", gpo_t, gpost_r, [], ["gpo_t"], "gpo_t")
    dma("sync", lam_t.rearrange("p a b -> p (a b)"), lam_v, [], ["lam_t"], "lam_t")
    dma("sync", gs, gsub_r, [], ["gs"], "gs")
    dma("sync", lvec.rearrange("p a b -> p (a b)"), lru_vec, [], ["lvec"], "lvec")

    act(cs_a, cs_t, AF.Tanh, ["cs_t"], ["cs_a"], scale=0.5)
    dve("scalar_tensor_tensor", ["cs_a", "cs_t"], ["cs_a"], cs_a, cs_a, 1.0, cs_t, ALU.add, ALU.mult)
    dve("tensor_scalar", ["cs_a"], ["cs_a"], cs_a, cs_a, 0.5, None, ALU.mult)
    dve("tensor_copy", ["cs_a"], ["cs_b"], cs_b, cs_a)
    dve("tensor_copy", ["cs_a"], ["csb"], csb, bcast_last(cs_a.rearrange("p a b -> p (a b)"), 128).rearrange(
        "p (a b) c -> p a b c", a=8))
    wada_v = w_ada.rearrange("(kc p) n -> p kc n", p=128)
    def mod_piece(pi):
        slot = pi % 4
        dma("gpsimd", wa[slot], wada_v[:, :, pi * 256:(pi + 1) * 256], [], [("wa", slot)], ("wa", slot))
        if pi < 8:
            for o2 in range(2):
                oc = pi * 2 + o2
                for kc in range(8):
                    mm(bank(7)[:, oc * 2:oc * 2 + 2], wa[slot][:, kc, o2 * 128:(o2 + 1) * 128], cs_b[:, kc, :],
                       kc == 0, kc == 7, [("wa", slot), "cs_b"], [("ps", 7)])
        else:
            q_ = pi - 8
            for j in range(2):
                for kc in range(8):
                    mm(bank(j)[:, 0:256], csb[:, kc, j, :], wa[slot][:, kc, :], kc == 0, kc == 7, [("wa", slot), "csb"], [("ps", j)])
                gsl = Gs[:, j, q_ * 256:(q_ + 1) * 256]
                dve("tensor_tensor", [("ps", j), "bg_t"], ["Gs"], gsl, bank(j)[:, 0:256], bg_t[:, q_ * 256:(q_ + 1) * 256], ALU.add)
                dve("tensor_tensor", ["Gs", "gpo_t"], ["Gs"], gsl, gsl, gpo_t[:, q_ * 256:(q_ + 1) * 256], ALU.mult)
        if pi == 7:
            dve("tensor_tensor", [("ps", 7), "bss"], ["modAB"], modAB, bank(7)[:, 0:32].rearrange("p (a b) -> p a b", b=2),
                bcast_last(bss, 2), ALU.add)
            dve("tensor_scalar", ["modAB"], ["modAB"], modAB[:, 8:16, :], modAB[:, 8:16, :], 1.0, None, ALU.add)
            dve("tensor_tensor", ["modAB", "gpre_t"], ["modAB"], modAB[:, 8:16, :], modAB[:, 8:16, :],
                bcast_last(gpre_t, 2), ALU.mult)
    for pi in range(8):
        mod_piece(pi)
    dve("tensor_tensor", ["lam_t"], ["lam_p"], lam_p, lam_t[:, 0:2, :], lam_t[:, 2:4, :], ALU.mult)
    dve("tensor_reduce", ["lam_p"], ["small"], small[:, 0:2], lam_p, AX.X, ALU.add)
    act(small[:, 2:4], small[:, 0:2], AF.Exp, ["small"], ["small"])
    dve("tensor_tensor", ["small"], ["small"], small[:, 4:5], small[:, 2:3], small[:, 3:4], ALU.subtract)
    dve("tensor_scalar", ["small"], ["nlam"], nlam[:, 0:1], small[:, 4:5], LAM_INIT, -1.0, ALU.add, ALU.mult)
    dve("tensor_scalar", ["gs"], ["gs"], gs, gs, (1.0 - LAM_INIT) * 0.5, None, ALU.mult)
    dve("tensor_scalar", ["lvec"], ["lcon"], lcon[:, :, 0:4], lvec[:, :, 6:10], 0.5, None, ALU.mult)
    lsp = ar.alloc([8, 2], F32)
    act(lsp, lvec[:, :, 10:12], AF.Exp, ["lvec"], ["lsp"], scale=-1.0)
    act(lsp, lsp, AF.Ln, ["lsp"], ["lsp"], bias=1.0)
    dve("tensor_scalar", ["lsp"], ["lcon"], lcon[:, :, 4:6], lsp, -4.0, None, ALU.mult)

    dma("gpsimd", cosT, rope_cos, [], ["cosT"], "cosT")
    dma("gpsimd", sinT, rope_ssin, [], ["sinT"], "sinT")

    xt = [ar.alloc([1024], F32) for _ in range(6)]
    xn = [ar.alloc([1024], BF16) for _ in range(4)]
    ssq = ar.alloc([40], F32)
    rstd = ar.alloc([40], F32)
    def b_front(g):
        pb = (g % 2) * 4
        for i in range(4):
            ti = g * 4 + i
            src = xs[ti * 128:(ti + 1) * 128, :] if g < 8 else xp[i * 128:(i + 1) * 128, :]
            dma("sync", xt[ti % 6], src, [], [("xt", ti % 6)], ("xt", ti % 6))
            act(xn[i], xt[ti % 6], AF.Square, [("xt", ti % 6)], [("xn", i), ("ssq", ti)], accum=ssq[:, ti:ti + 1])
            act(ssq[:, ti:ti + 1], ssq[:, ti:ti + 1], AF.Sqrt, [("ssq", ti)], [("ssq", ti)], scale=1.0 / 1024, bias=EPS)
            dve("reciprocal", [("ssq", ti)], [("rstd", ti)], rstd[:, ti:ti + 1], ssq[:, ti:ti + 1])
            dve("tensor_scalar", [("xt", ti % 6), ("rstd", ti)], [("xn", i)], xn[i], xt[ti % 6], rstd[:, ti:ti + 1], None, ALU.mult)
            for kc in range(8):
                b_ = pb + kc // 2
                tr(bank_bf(b_)[:, (kc % 2) * 512 + i * 128:(kc % 2) * 512 + (i + 1) * 128], xn[i][:, kc * 128:(kc + 1) * 128],
                   [("xn", i)], [("ps", b_)])

    def b_back(g):
        j = 0 if g < 8 else 1
        pb = (g % 2) * 4
        for kc in range(8):
            b_ = pb + kc // 2
            src = bank_bf(b_)[:, (kc % 2) * 512:(kc % 2) * 512 + 512]
            dst = hT[:, kc, g * 512:(g + 1) * 512]
            if (kc // 2) != 1:
                dve("tensor_scalar", [("ps", b_), "modAB"], [("hT", g)], dst, src, modAB[:, 8 + kc, j:j + 1],
                    modAB[:, kc, j:j + 1], ALU.mult, ALU.add)
            else:
                act(dst, src, AF.Identity, [("ps", b_), "modAB"], [("hT", g)], scale=modAB[:, 8 + kc, j:j + 1],
                    bias=modAB[:, kc, j:j + 1])

    b_front(0)
    for g in range(1, 9):
        b_front(g)
        b_back(g - 1)
    b_back(8)
    for pi in range(8, 12):
        mod_piece(pi)
    dma("sync", gs_d, Gs.rearrange("p a b -> p (a b)"), ["Gs"], [], "gs_d")
    P.set_fence()

    ar.off = cbase
    kT = ar.alloc([512 + T_S], BF16)
    qT = ar.alloc([T_OWN], BF16)
    vh = ar.alloc([36, 130], BF16)
    ckb = ar.alloc([4, 128], BF16)
    kTp = ar.alloc([T_P], BF16)
    qTp = ar.alloc([T_P], BF16)
    vp = ar.alloc([4, 130], BF16)
    gatt = ar.alloc([20, 128], F32)
    nk_st = ar.alloc([4, 128], F32)
    nv_st = ar.alloc([4, 128], F32)
    Et = [ar.alloc([2, 512], BF16) for _ in range(3)]
    rt1 = [ar.alloc([512], F32) for _ in range(2)]
    qb = [ar.alloc([512], BF16) for _ in range(2)]
    rt2 = [ar.alloc([512], F32) for _ in range(2)]
    o_all = ar.alloc([20, 128], F32)
    o1 = ar.alloc([4, 128], F32)
    rden = ar.alloc([8], F32)
    avs = ar.alloc([8, 129], F32)
    ssj = ar.alloc([3, 4], F32)
    att_b = ar.alloc([12, 128], BF16)
    mixo = [ar.alloc([512], BF16) for _ in range(3)]

    pool("memset", [], ["vh1"], vh[:, :, 128:130], 1.0)
    pool("memset", [], ["vp1"], vp[:, :, 128:130], 1.0)

    ck_v = ck.rearrange("(t p) f -> p t f", p=128)
    cv_v = cv.rearrange("(t p) f -> p t f", p=128)
    nk_v = nk.rearrange("(t p) f -> p t f", p=128)
    nv_v = nv.rearrange("(t p) f -> p t f", p=128)

    ecount = [0]
    scount = [0]

    def attention_jobs(jobs, h):
        items = []
        for ji, jb in enumerate(jobs):
            for kt in range(len(jb[0])):
                items.append((ji, kt))

        def region(r):
            return bank(4 + r // 3)[:, (r % 3) * 129:(r % 3) * 129 + 129]

        def issue_S(idx):
            ji, kt = items[idx]
            kT_tiles, v_tiles, q_ap, nq, o_tile0, rd_extra = jobs[ji]
            ss_ = idx % 2
            es = idx % 3
            ka, kkey = kT_tiles[kt]
            for m in range(2):
                mm(bank(2 * ss_ + m)[:, 0:nq], ka[m * 64:(m + 1) * 64, :], q_ap[m * 64:(m + 1) * 64, :], True, True,
                   [kkey] + rd_extra, [("ps", 2 * ss_ + m)])
            if nq == 512:
                act(Et[es].rearrange("p a b -> p (a b)"), psum[:, 2 * ss_:2 * ss_ + 2, :].rearrange("p a b -> p (a b)"), AF.Exp,
                    [("ps", 2 * ss_), ("ps", 2 * ss_ + 1)], [("E", es)], scale=0.125)
            else:
                act(Et[es][:, :, 0:nq], psum[:, 2 * ss_:2 * ss_ + 2, 0:nq], AF.Exp, [("ps", 2 * ss_), ("ps", 2 * ss_ + 1)], [("E", es)], scale=0.125)

        def issue_AV(idx):
            ji, kt = items[idx]
            kT_tiles, v_tiles, q_ap, nq, o_tile0, rd_extra = jobs[ji]
            nqs = nq // 128
            nkt = len(kT_tiles)
            es = idx % 3
            va, vkey = v_tiles[kt]
            for m in range(2):
                for qs in range(nqs):
                    r = m * nqs + qs
                    mm(region(r), Et[es][:, m, qs * 128:(qs + 1) * 128], va, (kt == 0 and r % 3 == 0), kt == nkt - 1,
                       [("E", es), vkey], [("ps", 4 + r // 3)], skip=True)
            if kt == nkt - 1:
                finish(ji)

        def finish(ji):
            kT_tiles, v_tiles, q_ap, nq, o_tile0, rd_extra = jobs[ji]
            nqs = nq // 128
            nreg = 2 * nqs
            nb = (nreg + 2) // 3
            for b_ in range(nb):
                n_in = min(3, nreg - 3 * b_)
                dve("tensor_copy", [("ps", 4 + b_)], ["avs"], avs[:, 3 * b_:3 * b_ + n_in, :].rearrange("p a b -> p (a b)"),
                    bank(4 + b_)[:, 0:n_in * 129])
            dve("reciprocal", ["avs"], ["rden"], rden[:, 0:nreg], avs[:, 0:nreg, 128])
            dve("tensor_scalar", ["rden", "nlam"], ["rden"], rden[:, nqs:nreg], rden[:, nqs:nreg], nlam[:, 0:1], None, ALU.mult)
            for qs in range(nqs):
                r1, r2 = qs, nqs + qs
                dve("tensor_scalar", ["avs", "rden"], ["o1"], o1[:, qs, :], avs[:, r1, 0:128], rden[:, r1:r1 + 1], None, ALU.mult)
                dve("scalar_tensor_tensor", ["avs", "rden", "o1"], [("o", o_tile0 + qs)], o_all[:, o_tile0 + qs, :],
                    avs[:, r2, 0:128], rden[:, r2:r2 + 1], o1[:, qs, :], ALU.mult, ALU.add)
            sl = ji % 3
            okk = [("o", o_tile0 + qs) for qs in range(nqs)]
            dve("tensor_tensor", okk + ["o1"], ["o1"], o1[:, 0:nqs, :], o_all[:, o_tile0:o_tile0 + nqs, :], o_all[:, o_tile0:o_tile0 + nqs, :], ALU.mult)
            dve("tensor_reduce", ["o1"], [("ssj", sl)], ssj[:, sl, 0:nqs], o1[:, 0:nqs, :], AX.X, ALU.add)

            def part2():
                act(ssj[:, sl, 0:nqs], ssj[:, sl, 0:nqs], AF.Ln, [("ssj", sl)], [("ssj", sl)], scale=1.0 / 128, bias=EPS)
                act(ssj[:, sl, 0:nqs], ssj[:, sl, 0:nqs], AF.Exp, [("ssj", sl)], [("ssj", sl)], scale=-0.5)

            def part3():
                for qs in range(nqs):
                    t_ = o_tile0 + qs
                    dve("scalar_tensor_tensor", [("o", t_), ("ssj", sl), ("gatt", t_ // 4)], [("att", sl)], att_b[:, sl * 4 + qs, :], o_all[:, t_, :],
                        ssj[:, sl, qs:qs + 1], gatt[:, t_, :], ALU.mult, ALU.mult)
                for qs in range(nqs):
                    tr(bank_bf(7)[:, qs * 128:(qs + 1) * 128], att_b[:, sl * 4 + qs, :], [("att", sl)], [("ps", 7)])
                dve("tensor_copy", [("ps", 7)], [("mixo", sl)], mixo[sl][:, 0:nq], bank_bf(7)[:, 0:nq])
                dma("sync", mix_d[h * 128:(h + 1) * 128, o_tile0 * 128:o_tile0 * 128 + nq], mixo[sl][:, 0:nq], [("mixo", sl)], [], ("mixo", sl))

            pending.append([cur[0] + 8, part2])
            pending.append([cur[0] + 11, part3])

        pending = []
        cur = [0]
        n = len(items)
        issue_S(0)
        if n > 1:
            issue_S(1)
        for idx in range(n):
            cur[0] = idx
            if idx + 2 < n:
                issue_S(idx + 2)
            issue_AV(idx)
            while pending and pending[0][0] <= idx:
                pending.pop(0)[1]()
        return pending

    pend_prev = [[]]
    for h in range(8):
        w = wA[h % 2]
        wk = ("wA", h % 2)
        if h + 1 < 8:
            load_head_w(h + 1)
        dma("gpsimd", ckb, ck_v[:, :, h * 128:(h + 1) * 128], [], ["ckb"], "ckb")
        dma("gpsimd", vh[:, 0:4, 0:128], cv_v[:, :, h * 128:(h + 1) * 128], [], [("vh", "c")], ("vh", "c"))
        for t in range(4):
            tr(bank_bf(7)[:, t * 128:(t + 1) * 128], ckb[:, t, :], ["ckb"], [("ps", 7)])
        dve("tensor_copy", [("ps", 7)], [("kT", "c")], kT[:, 0:512], bank_bf(7)[:, 0:512])
        note('h%d rope' % h)
        def rope_finish(gi):
            isq = gi >= 8
            g = gi - 8 if isq else gi
            pb = (gi % 2) * 2
            s_ = gi % 2
            mm(bank(pb + 1), pmat, qb[s_], True, True, [("qb", s_), "pmat"], [("ps", pb + 1)])
            dve("tensor_tensor", [("ps", pb + 1), "sinT"], [("rt2", s_)], rt2[s_], bank(pb + 1), sinT[:, g * 512:(g + 1) * 512], ALU.mult)
            if isq:
                pool("tensor_tensor", [("rt1", s_), ("rt2", s_)], [("qT", g)], qT[:, g * 512:(g + 1) * 512], rt1[s_], rt2[s_], ALU.add)
            else:
                pool("tensor_tensor", [("rt1", s_), ("rt2", s_)], [("kT", g)], kT[:, 512 + g * 512:512 + (g + 1) * 512], rt1[s_], rt2[s_], ALU.add)

        for gi in range(12):
            isq = gi >= 8
            g = gi - 8 if isq else gi
            c0 = 0 if isq else 128
            pb = (gi % 2) * 2
            s_ = gi % 2
            for kc in range(8):
                mm(bank(pb), w[:, kc, c0:c0 + 128], hT[:, kc, g * 512:(g + 1) * 512], kc == 0, kc == 7, [wk, ("hT", g)], [("ps", pb)])
            if gi >= 1:
                rope_finish(gi - 1)
            act(qb[s_], bank(pb), AF.Copy, [("ps", pb)], [("qb", s_)])
            dve("tensor_tensor", [("ps", pb), "cosT"], [("rt1", s_)], rt1[s_], bank(pb), cosT[:, g * 512:(g + 1) * 512], ALU.mult)
        rope_finish(11)
        while pend_prev[0]:
            pend_prev[0].pop(0)[1]()
        note('h%d v' % h)
        for i0 in range(0, 16, 2):
            pb = (i0 // 2) % PBMOD
            for ii in range(2):
                i = i0 + ii
                for kc in range(8):
                    mm(bank(pb)[:, ii * 256:(ii + 1) * 256], hT[:, kc, i * 128:(i + 1) * 128], w[:, kc, 256:512], kc == 0 and ii == 0, kc == 7,
                       [wk, ("hT", i // 4)], [("ps", pb)], skip=True)
            pv = bank(pb).rearrange("p (a b) -> p a b", a=2)
            dve("tensor_copy", [("ps", pb)], [("vh", (4 + i0) // 4), "vhx"], vh[:, 4 + i0:6 + i0, 0:128], pv[:, :, 0:128])
            act(gatt[:, i0:i0 + 2, :], pv[:, :, 128:256], AF.Tanh if EXPT != 'B' else AF.Identity, [("ps", pb)] + (["vhx"] if EXPT == 'A' else []), [("gatt", i0 // 4)], scale=0.5)
            dve("scalar_tensor_tensor", [("gatt", i0 // 4), ("ps", pb)], [("gatt", i0 // 4)], gatt[:, i0:i0 + 2, :], gatt[:, i0:i0 + 2, :], 1.0,
                pv[:, :, 128:256], ALU.add, ALU.mult)
        for i0 in range(16, 32, 4):
            pb = (i0 // 4) % 4
            for ii in range(4):
                i = i0 + ii
                for kc in range(8):
                    mm(bank(pb)[:, ii * 128:(ii + 1) * 128], hT[:, kc, i * 128:(i + 1) * 128], w[:, kc, 256:384], kc == 0 and ii == 0, kc == 7,
                       [wk, ("hT", i // 4)], [("ps", pb)], skip=True)
            act(vh[:, 4 + i0:8 + i0, 0:128], bank(pb).rearrange("p (a b) -> p a b", a=4), AF.Copy, [("ps", pb)], [("vh", (4 + i0) // 4)])
        note('h%d prompt' % h)
        for kc in range(8):
            mm(bank(0), w[:, kc, 0:128], hT[:, kc, T_S:T_S + T_P], kc == 0, kc == 7, [wk, ("hT", 8)], [("ps", 0)])
        for kc in range(8):
            mm(bank(1), w[:, kc, 128:256], hT[:, kc, T_S:T_S + T_P], kc == 0, kc == 7, [wk, ("hT", 8)], [("ps", 1)])
        act(qTp, bank(0), AF.Copy, [("ps", 0)], ["qTp"])
        dve("tensor_copy", [("ps", 1)], ["kTp"], kTp, bank(1))
        for i in range(4):
            pb = 2 + i % 2
            for kc in range(8):
                mm(bank(pb)[:, 0:384], hT[:, kc, T_S + i * 128:T_S + (i + 1) * 128], w[:, kc, 128:512], kc == 0, kc == 7,
                   [wk, ("hT", 8)], [("ps", pb)])
            dve("tensor_copy", [("ps", pb)], ["nk_st"], nk_st[:, i, :], bank(pb)[:, 0:128])
            dve("tensor_copy", [("ps", pb)], ["nv_st"], nv_st[:, i, :], bank(pb)[:, 128:256])
            act(vp[:, i, 0:128], bank(pb)[:, 128:256], AF.Copy, [("ps", pb)], [("vp", i // 2)])
            act(gatt[:, 16 + i, :], bank(pb)[:, 256:384], AF.Tanh, [("ps", pb)], [("gatt", 4)], scale=0.5)
            dve("scalar_tensor_tensor", [("gatt", 4), ("ps", pb)], [("gatt", 4)], gatt[:, 16 + i, :], gatt[:, 16 + i, :], 1.0,
                bank(pb)[:, 256:384], ALU.add, ALU.mult)
        dma("gpsimd", nk_v[:, :, h * 128:(h + 1) * 128], nk_st, ["nk_st"], [], "nk_out")
        dma("gpsimd", nv_v[:, :, h * 128:(h + 1) * 128], nv_st, ["nv_st"], [], "nv_out")
        note('h%d attn' % h)
        kt_list = [(kT[:, t * 128:(t + 1) * 128], ("kT", "c")) for t in range(4)] + \
                  [(kT[:, 512 + i * 128:512 + (i + 1) * 128], ("kT", i // 4)) for i in range(32)]
        v_list = [(vh[:, t, 0:129], ("vh", "c")) for t in range(4)] + [(vh[:, 4 + i, 0:129], ("vh", (4 + i) // 4)) for i in range(32)]
        jobs = []
        for qg in range(4):
            jobs.append((kt_list, v_list, qT[:, qg * 512:(qg + 1) * 512], 512, qg * 4, [("qT", qg), "vh1"]))
        for s in range(2):
            ktl = [(kTp[:, s * 256 + t * 128:s * 256 + (t + 1) * 128], "kTp") for t in range(2)]
            vl = [(vp[:, s * 2 + t, 0:129], ("vp", s)) for t in range(2)]
            jobs.append((ktl, vl, qTp[:, s * 256:(s + 1) * 256], 256, 16 + s * 2, ["qTp", "vp1"]))
        gk = [("gatt", i) for i in range(5)]
        pool("tensor_tensor", gk + ["gs"], gk, gatt, gatt, bcast_mid(gs, 20), ALU.mult)
        pend_prev[0] = attention_jobs(jobs, h)
        note('h%d subln' % h)
    while pend_prev[0]:
        pend_prev[0].pop(0)[1]()
    P.set_fence()

    ar.off = xbase
    wL = [ar.alloc([8, 256], BF16) for _ in range(2)]
    wG = ar.alloc([8, 4, 128], BF16)
    xl = ar.alloc([T_S + 4 + 2 * 260], BF16)
    sgl = ar.alloc([NTOK], F32)
    NBT = 1024
    ubuf = [ar.alloc([NBT], F32) for _ in range(2)] + [ar.alloc([256], F32) for _ in range(2)]
    ubb = [ar.alloc([NBT], BF16) for _ in range(2)] + [ar.alloc([256], BF16) for _ in range(2)]
    Abk = [ar.alloc([NBT], F32) for _ in range(2)] + [ar.alloc([256], F32) for _ in range(2)]
    Ibk = [ar.alloc([NBT], F32) for _ in range(2)] + [ar.alloc([256], F32) for _ in range(2)]
    Mbk = [ar.alloc([NBT], F32) for _ in range(2)] + [ar.alloc([256], F32) for _ in range(2)]
    Afw = ar.alloc([T_OWN], F32)
    Ifw = ar.alloc([T_OWN], F32)
    Mfw = ar.alloc([T_OWN], F32)
    hbs = ar.alloc([NTOK], F32)
    hfs = ar.alloc([NTOK], F32)
    wD1 = ar.alloc([5, 128], BF16)
    wD = [wD1, wD1]
    lru_b = ar.alloc([NTOK], BF16)

    dma("gpsimd", wG.rearrange("p a b c -> p (a b c)"), w_gate, [], ["wG"], "wG")
    pool("memset", [], [("xl", g) for g in range(9)], xl, 0.0)
    wlru_v = [w_lru[j].rearrange("(kc p) n -> p kc n", p=128) for j in range(8)]
    dve("tensor_scalar", ["lvec"], ["lcon"], lcon[:, :, 6:8], lvec[:, :, 12:14], 2.0, None, ALU.mult)

    def load_lru_w(j):
        dma("gpsimd", wL[j % 2], wlru_v[j], [], [("wL", j % 2)], ("wL", j % 2))

    gcount = [0]
    ccount = [0]

    class Batch:
        def __init__(self, st_, xoff, n, fwd_off):
            self.set = st_
            self.xoff = xoff
            self.n = n
            self.L = 512 if n >= 512 else n
            self.nch = n // self.L
            self.fwd_off = fwd_off

        def kb(self, nm, c):
            return (nm, self.set, c)

        def kf(self, nm, c):
            return (nm, (self.fwd_off + c * self.L) // 512)

        def allk(self, nm, fwd):
            return [self.kf(nm, c) if fwd else self.kb(nm, c) for c in range(self.nch)]

    def xl_keys(x0, L):
        if x0 >= 4100:
            return [("xl", 8)]
        g0 = max(0, (x0 - 2) // 512)
        g1 = min(7, (x0 + L + 1) // 512)
        return [("xl", g) for g in range(g0, g1 + 1)]

    def stage_A1(j, B):
        L = B.L
        for c in range(B.nch):
            u = ubuf[B.set][:, c * L:(c + 1) * L]
            x0 = B.xoff + c * L
            uk = B.kb("u", c)
            cb_ = 4 + ccount[0] % 4
            ccount[0] += 1
            for tp in range(5):
                mm(bank(cb_)[:, 0:L], wD[j % 2][:, tp, :], xl[:, x0 + tp:x0 + tp + L], tp == 0, tp == 4, xl_keys(x0, L) + ["wD"], [("ps", cb_)])
            dve("tensor_scalar", [("ps", cb_), "lvec"], [uk], u, bank(cb_)[:, 0:L], lvec[:, j, 5:6], None, ALU.add)
            dve("tensor_copy", [uk], [B.kb("ubb", c)], ubb[B.set][:, c * L:(c + 1) * L], u)

    def stage_A2(j, B):
        L = B.L
        for c in range(B.nch):
            u = ubuf[B.set][:, c * L:(c + 1) * L]
            uk = B.kb("u", c)
            ubc = ubb[B.set][:, c * L:(c + 1) * L]
            dirs = [1] if B.fwd_off is None else [0, 1]
            for d in dirs:
                gi = gcount[0] % 2
                gcount[0] += 1
                pr, pi_ = gi * 2, gi * 2 + 1
                mm(bank(pr)[:, 0:L], wG[:, j, 2 * d, :], ubc, True, True, ["wG", B.kb("ubb", c)], [("ps", pr)])
                mm(bank(pi_)[:, 0:L], wG[:, j, 2 * d + 1, :], ubc, True, True, ["wG", B.kb("ubb", c)], [("ps", pi_)])
                if d == 1:
                    A_, I_, M_ = (t_[B.set][:, c * L:(c + 1) * L] for t_ in (Abk, Ibk, Mbk))
                    ka, ki, km = B.kb("A", c), B.kb("I", c), B.kb("M", c)
                else:
                    o = B.fwd_off + c * L
                    A_, I_, M_ = Afw[:, o:o + L], Ifw[:, o:o + L], Mfw[:, o:o + L]
                    ka, ki, km = B.kf("Af", c), B.kf("If", c), B.kf("Mf", c)
                act(A_, bank(pr)[:, 0:L], AF.Tanh, [("ps", pr), "lcon"], [ka], scale=0.5, bias=lcon[:, j, d:d + 1])
                act(A_, A_, AF.Exp, [ka, "lcon"], [ka], scale=lcon[:, j, 4 + d:5 + d], bias=lcon[:, j, 4 + d:5 + d])
                act(I_, bank(pi_)[:, 0:L], AF.Tanh, [("ps", pi_), "lcon"], [ki], scale=0.5, bias=lcon[:, j, 2 + d:3 + d])
                pool("tensor_tensor", [ka], [km], M_, A_, A_, ALU.mult)
                dve("scalar_tensor_tensor", [ki, uk], [ki], I_, I_, 1.0, u, ALU.add, ALU.mult)

    def stage_A(j, B):
        stage_A1(j, B)
        stage_A2(j, B)

    def arrs(B, d):
        if d == 1:
            return (Abk[B.set][:, 0:B.n], Ibk[B.set][:, 0:B.n], Mbk[B.set][:, 0:B.n], B.allk("A", False), B.allk("I", False), B.allk("M", False))
        o = B.fwd_off
        return (Afw[:, o:o + B.n], Ifw[:, o:o + B.n], Mfw[:, o:o + B.n], B.allk("Af", True), B.allk("If", True), B.allk("Mf", True))

    def stage_S(B):
        for d in ([1] if B.fwd_off is None else [0, 1]):
            A_, I_, M_, ka, ki, km = arrs(B, d)
            act(M_, M_, AF.Sqrt, km, km, scale=-1.0, bias=1.0)

    def stage_C(B):
        for d in ([1] if B.fwd_off is None else [0, 1]):
            A_, I_, M_, ka, ki, km = arrs(B, d)
            dve("tensor_tensor", ki + km, ki, I_, I_, M_, ALU.mult)

    def scan_b(B, init_ap, init_key, out_ap, out_key):
        A_, I_, M_, ka, ki, km = arrs(B, 1)
        dve("tensor_tensor_scan", ka + ki + [init_key], [out_key], out_ap[:, ::-1], A_[:, ::-1], I_[:, ::-1], init_ap, ALU.mult, ALU.add)

    _save = ar.off
    ar.off = hT_off
    wO = ar.alloc([16, 1024], BF16)
    ar.off = _save
    wout_v = w_out.rearrange("(c p) n -> p c n", p=128)

    load_lru_w(0)
    for j in range(8):
        w = wL[j % 2]
        wk = ("wL", j % 2)
        if j + 1 < 8:
            load_lru_w(j + 1)
        for tp in range(5):
            dve("tensor_scalar", ["ident", "lvec"], ["wD"], wD[j % 2][:, tp, :], ident, lvec[:, j, tp:tp + 1], None, ALU.mult)
        B4 = Batch(2, 4100, 256, 0)
        B5 = Batch(3, 4360, 256, 512)
        for g in [8, 7, 6, 5, 4, 3, 2, 1, 0]:
            pb = 4 + g % 4
            for kc in range(8):
                mm(bank(pb), w[:, kc, 0:128], hT[:, kc, g * 512:(g + 1) * 512], kc == 0, kc == 7, [wk, ("hT", g)], [("ps", pb)])
            if g < 8:
                act(xl[:, 2 + g * 512:2 + (g + 1) * 512], bank(pb), AF.Copy, [("ps", pb)], [("xl", g)])
            else:
                for s in range(2):
                    act(xl[:, 4102 + s * 260:4102 + s * 260 + 256], bank(pb)[:, s * 256:(s + 1) * 256], AF.Copy, [("ps", pb)], [("xl", 8)])
                stage_A1(j, B4)
                stage_A1(j, B5)
        for gi, g in enumerate([0, 1, 2, 3, 8]):
            pb = gi % 4
            for kc in range(8):
                mm(bank(pb), w[:, kc, 128:256], hT[:, kc, g * 512:(g + 1) * 512], kc == 0, kc == 7, [wk, ("hT", g)], [("ps", pb)])
            sg = sgl[:, gi * 512:(gi + 1) * 512]
            act(sg, bank(pb), AF.Tanh, [("ps", pb)], [("sgl", gi)], scale=0.5)
            dve("scalar_tensor_tensor", [("sgl", gi), ("ps", pb)], [("sgl", gi)], sg, sg, 1.0, bank(pb), ALU.add, ALU.mult)
        if j == 7:
            for c4 in range(4):
                dma("gpsimd", wO[:, c4 * 4:(c4 + 1) * 4, :], wout_v[:, c4 * 4:(c4 + 1) * 4, :], [], [("wO", c4)] + [("hT", g) for g in range(9)],
                    ("wO", c4))
        B0 = Batch(0, 3072, 1024, None)
        B1 = Batch(1, 2048, 1024, None)
        B2 = Batch(0, 1024, 1024, 1024)
        B3 = Batch(1, 0, 1024, 0)
        carry = small[:, 8:9]
        stage_A1(j, B0)
        stage_A1(j, B1)
        stage_A2(j, B4)
        stage_A2(j, B5)
        stage_A2(j, B0)
        stage_A2(j, B1)
        stage_S(B4)
        stage_S(B5)
        stage_S(B0)
        stage_S(B1)
        for s, B in enumerate((B4, B5)):
            stage_C(B)
            p0 = T_OWN + s * 256
            scan_b(B, zero_c[:, 0:1], "zero_c", hbs[:, p0:p0 + 256], ("hbs", 2 + s))
            A_, I_, M_, ka, ki, km = arrs(B, 0)
            dve("tensor_tensor_scan", ka + ki + ["zero_c"], [("hfs", 2 + s)], hfs[:, p0:p0 + 256], A_, I_, zero_c[:, 0:1], ALU.mult, ALU.add)
            dve("tensor_scalar", [("hfs", 2 + s)], ["st_out"], st_out[:, j, 2 * s:2 * s + 1], hfs[:, p0 + 255:p0 + 256], 0.5, None, ALU.mult)
            dve("tensor_scalar", [("hbs", 2 + s)], ["st_out"], st_out[:, j, 2 * s + 1:2 * s + 2], hbs[:, p0:p0 + 1], 0.5, None, ALU.mult)
            pool("tensor_tensor", [("hfs", 2 + s), ("hbs", 2 + s)], [("hfs", 2 + s)], hfs[:, p0:p0 + 256], hfs[:, p0:p0 + 256], hbs[:, p0:p0 + 256], ALU.add)
            dve("scalar_tensor_tensor", [("hfs", 2 + s), ("sgl", 4)], [("lru_b", 1)], lru_b[:, p0:p0 + 256], hfs[:, p0:p0 + 256], 0.25,
                sgl[:, p0:p0 + 256], ALU.mult, ALU.mult)
        dma("sync", mix_d[(8 + j) * 128:(9 + j) * 128, T_OWN:NTOK], lru_b[:, T_OWN:NTOK], [("lru_b", 1)], [], ("lru_b", 1))
        stage_C(B0)
        scan_b(B0, lcon[:, j, 7:8], "lcon", Mbk[0], ("M", 0, 0))
        pool("tensor_copy", [("M", 0, 0)], ["carry"], carry, Mbk[0][:, 0:1])
        stage_A1(j, B2)
        stage_C(B1)
        scan_b(B1, carry, "carry", Mbk[1], ("M", 1, 0))
        pool("tensor_copy", [("M", 1, 0)], ["carry"], carry, Mbk[1][:, 0:1])
        stage_A1(j, B3)
        stage_A2(j, B2)
        stage_A2(j, B3)
        stage_S(B2)
        stage_S(B3)
        stage_C(B2)
        scan_b(B2, carry, "carry", hbs[:, 1024:2048], ("hbs", 1))
        stage_C(B3)
        scan_b(B3, hbs[:, 1024:1025], ("hbs", 1), hbs[:, 0:1024], ("hbs", 0))
        fk = [("Af", c) for c in range(4)] + [("If", c) for c in range(4)]
        dve("tensor_tensor_scan", fk + ["lcon"], [("hfs", 0, "a"), ("hfs", 0, "b")], hfs[:, 0:T_OWN], Afw, Ifw, lcon[:, j, 6:7], ALU.mult, ALU.add)
        pool("tensor_tensor", [("hfs", 0, "b"), ("hbs", 1)], [("hfs", 0, "b")], hfs[:, 1024:T_OWN], hfs[:, 1024:T_OWN], hbs[:, 1024:T_OWN], ALU.add)
        dve("tensor_tensor", [("hfs", 0, "a"), ("hbs", 0)], [("hfs", 0, "a")], hfs[:, 0:1024], hfs[:, 0:1024], hbs[:, 0:1024], ALU.add)
        sk = [("sgl", gi) for gi in range(4)]
        dve("scalar_tensor_tensor", [("hfs", 0, "a")] + sk, [("lru_b", 0)], lru_b[:, 0:1024], hfs[:, 0:1024], 0.25, sgl[:, 0:1024], ALU.mult, ALU.mult)
        dve("scalar_tensor_tensor", [("hfs", 0, "b")] + sk, [("lru_b", 0)], lru_b[:, 1024:T_OWN], hfs[:, 1024:T_OWN], 0.25, sgl[:, 1024:T_OWN], ALU.mult, ALU.mult)
        dma("sync", mix_d[(8 + j) * 128:(9 + j) * 128, 0:T_OWN], lru_b[:, 0:T_OWN], [("lru_b", 0)], [], ("lru_b", 0))
    dma("sync", nst, st_out.rearrange("p a b -> p (a b)"), ["st_out"], [], "nst")
    P.set_fence()

    ar.off = xbase
    mt = [ar.alloc([16, 128], BF16) for _ in range(2)]
    xr = [ar.alloc([1024], F32) for _ in range(2)]
    yt = [ar.alloc([1024], F32) for _ in range(2)]
    junk2 = ar.alloc([1024], BF16)
    ss_e = ar.alloc([20], F32)
    Gs = ar.alloc([2, 1024], F32)
    dma("sync", Gs.rearrange("p a b -> p (a b)"), gs_d, [], ["Gs"], "gs_ld")
    rs_e = ar.alloc([20], F32)
    mixd_v = mix_d.rearrange("(c p) t -> p c t", p=128)
    for i in range(20):
        sl = i % 2
        j = 0 if i < 16 else 1
        dma("sync", mt[sl], mixd_v[:, :, i * 128:(i + 1) * 128], [], [("mt", sl)], ("mt", sl))
        xsrc = xs[i * 128:(i + 1) * 128, :] if i < 16 else xp[(i - 16) * 128:(i - 15) * 128, :]
        dma("sync", xr[sl], xsrc, [], [("xr", sl)], ("xr", sl))
        pb = (i % 2) * 2
        for hc in range(2):
            for c in range(16):
                mm(bank(pb + hc), mt[sl][:, c, :], wO[:, c, hc * 512:(hc + 1) * 512], c == 0, c == 15, [("mt", sl), ("wO", c // 4)],
                   [("ps", pb + hc)])
        pso = psum[:, pb:pb + 2, :].rearrange("p a b -> p (a b)")
        act(junk2, pso, AF.Square, [("ps", pb), ("ps", pb + 1)], ["junk2", ("ss_e", i)], accum=ss_e[:, i:i + 1])
        act(ss_e[:, i:i + 1], ss_e[:, i:i + 1], AF.Sqrt, [("ss_e", i)], [("ss_e", i)], scale=1.0 / 1024, bias=EPS)
        dve("reciprocal", [("ss_e", i)], [("rs_e", i)], rs_e[:, i:i + 1], ss_e[:, i:i + 1])
        dve("scalar_tensor_tensor", [("ps", pb), ("ps", pb + 1), ("rs_e", i), "Gs"], [("yt", sl)], yt[sl], pso, rs_e[:, i:i + 1], Gs[:, j, :], ALU.mult, ALU.mult)
        pool("tensor_tensor", [("yt", sl), ("xr", sl)], [("yt", sl)], yt[sl], yt[sl], xr[sl], ALU.add)
        dst = y_s[i * 128:(i + 1) * 128, :] if i < 16 else y_p[(i - 16) * 128:(i - 15) * 128, :]
        dma("gpsimd", dst, yt[sl], [("yt", sl)], [], ("yt", sl))
    if STOP_AT is not None:
        P.cut = STOP_AT if STOP_AT > 10 else P.marks[STOP_AT]
    print('marks', P.marks, 'nops', len(P.ops), notes[:14])
    P.emit()
    return nc


_NC_CACHE = {}


def _rope_tables(rev):
    s = np.arange(T_S)
    t = (T_S - 1 - s) if rev else s
    row = (t // 64).astype(np.float32)
    col = (t % 64).astype(np.float32)
    inv = (np.float32(10000.0) ** (-np.arange(16, dtype=np.float32) * np.float32(2.0) / np.float32(32))).astype(np.float32)
    cos = np.zeros((128, T_S), np.float32)
    ssin = np.zeros((128, T_S), np.float32)
    for jj in range(128):
        d = jj % 64
        pos = row if d < 32 else col
        ang = (pos * inv[d % 16]).astype(np.float32)
        cos[jj] = np.cos(ang)
        sgn = -1.0 if (d % 32) < 16 else 1.0
        ssin[jj] = sgn * np.sin(ang)
    return cos, ssin


def kernel(x_prompt, x_sample, cache_k, cache_v, state_lru, c, c_ctx, w_ada, b_ada, g_pre, w_in,
           lambda_q1, lambda_k1, lambda_q2, lambda_k2, g_subln, conv_w, conv_b,
           w_rgate, b_rgate, w_igate, b_igate, lru_lambda, w_out, g_post):
    f = lambda a: np.ascontiguousarray(np.asarray(a, dtype=np.float32))
    x_prompt, x_sample, cache_k, cache_v, state_lru = map(f, (x_prompt, x_sample, cache_k, cache_v, state_lru))
    c, c_ctx, w_ada, b_ada, g_pre, w_in = map(f, (c, c_ctx, w_ada, b_ada, g_pre, w_in))
    w_out, g_post, g_subln, conv_w, conv_b = map(f, (w_out, g_post, g_subln, conv_w, conv_b))
    w_rgate, b_rgate, w_igate, b_igate, lru_lambda = map(f, (w_rgate, b_rgate, w_igate, b_igate, lru_lambda))
    lq1, lk1, lq2, lk2 = map(f, (lambda_q1, lambda_k1, lambda_q2, lambda_k2))

    if "nc" not in _NC_CACHE:
        _NC_CACHE["nc"] = build_program()
    nc = _NC_CACHE["nc"]

    W = w_in[0]
    perm = np.array([jj + 16 if (jj % 32) < 16 else jj - 16 for jj in range(128)])
    pmat = np.zeros((128, 128), np.float32)
    pmat[perm, np.arange(128)] = 1.0
    w_att = np.empty((8, 1024, 512), np.float32)
    w_lru = np.empty((8, 1024, 256), np.float32)
    for h in range(8):
        q = W[:, h * 128:(h + 1) * 128]
        k = W[:, 1024 + h * 128:1024 + (h + 1) * 128]
        v = W[:, 2048 + h * 128:2048 + (h + 1) * 128]
        g = W[:, 3072 + h * 128:3072 + (h + 1) * 128]
        w_att[h] = np.concatenate([q, k, v, g], axis=1)
        w_lru[h] = np.concatenate([W[:, 4096 + h * 128:4096 + (h + 1) * 128], W[:, 5120 + h * 128:5120 + (h + 1) * 128]], axis=1)
    col = lambda vec: np.ascontiguousarray(vec.reshape(-1, 128).T)
    rep = lambda vec: np.ascontiguousarray(np.broadcast_to(vec[None, :], (128, vec.shape[0])))
    bada_ss = col(b_ada[0, 0:2048])
    bada_g = rep(b_ada[0, 2048:3072])
    gpre_c = col(g_pre[0])
    gpost_r = rep(g_post[0])
    gsub_r = rep(g_subln[0])
    lam_v = np.ascontiguousarray(np.broadcast_to(np.concatenate([lq1[0], lq2[0], lk1[0], lk2[0]])[None, :], (128, 256)))
    ropes = {False: _rope_tables(False), True: _rope_tables(True)}

    in_maps = []
    for core in range(NCORES):
        b, half = core // 2, core % 2
        rev = half == 1
        df, db = (1, 0) if rev else (0, 1)
        xs = x_sample[b][::-1] if rev else x_sample[b]
        xpp = x_prompt[2 * core:2 * core + 2]
        if rev:
            xpp = xpp[:, ::-1]
        cs = np.stack([c[b].reshape(8, 128).T, c_ctx.reshape(8, 128).T], axis=-1)
        taps = np.zeros((5, 1024), np.float32)
        if rev:
            taps[0:4] = conv_w[0][::-1]
        else:
            taps[1:5] = conv_w[0]
        vecs = [taps[0], taps[1], taps[2], taps[3], taps[4], conv_b[0], b_rgate[0, df], b_rgate[0, db], b_igate[0, df], b_igate[0, db],
                lru_lambda[0, df], lru_lambda[0, db], state_lru[b, 0, df], state_lru[b, 0, db]]
        lru_vec = np.stack([col(vv) for vv in vecs], axis=-1)
        wg = np.stack([w_rgate[0, df], w_igate[0, df], w_rgate[0, db], w_igate[0, db]], axis=1)
        wg = np.ascontiguousarray(wg.transpose(2, 0, 1, 3)).reshape(128, 8 * 4 * 128)
        cos, ssin = ropes[rev]
        in_maps.append({
            "xs": np.ascontiguousarray(xs), "xp": np.ascontiguousarray(xpp.reshape(T_P, 1024)),
            "ck": np.ascontiguousarray(cache_k[b, 0].reshape(512, 1024)), "cv": np.ascontiguousarray(cache_v[b, 0].reshape(512, 1024)),
            "cs": np.ascontiguousarray(cs.reshape(128, 16)), "w_ada": w_ada[0], "bada_ss": bada_ss, "bada_g": bada_g,
            "gpre_c": gpre_c, "gpost_r": gpost_r, "w_att": w_att, "pmat": pmat, "rope_cos": cos, "rope_ssin": ssin, "w_lru": w_lru,
            "w_gate": wg, "lru_vec": np.ascontiguousarray(lru_vec.reshape(128, 8 * 14)), "w_out": w_out[0], "lam_v": lam_v, "gsub_r": gsub_r,
        })
    res = run_bass_kernel_spmd(nc, in_maps, core_ids=list(range(NCORES)))

    y_prompt = np.empty((16, 256, 1024), np.float32)
    y_sample = np.empty((4, 4096, 1024), np.float32)
    new_k = np.empty((16, 1, 256, 8, 128), np.float32)
    new_v = np.empty((16, 1, 256, 8, 128), np.float32)
    new_st = np.empty((16, 1, 2, 1024), np.float32)
    for core in range(NCORES):
        r = res.results[core]
        b, half = core // 2, core % 2
        rev = half == 1
        ys = r["y_s"]
        if rev:
            y_sample[b, 2048:4096] = ys[::-1]
        else:
            y_sample[b, 0:2048] = ys
        yp = r["y_p"].reshape(2, 256, 1024)
        nkk = r["nk"].reshape(2, 256, 8, 128)
        nvv = r["nv"].reshape(2, 256, 8, 128)
        if rev:
            yp, nkk, nvv = yp[:, ::-1], nkk[:, ::-1], nvv[:, ::-1]
        y_prompt[2 * core:2 * core + 2] = yp
        new_k[2 * core:2 * core + 2, 0] = nkk
        new_v[2 * core:2 * core + 2, 0] = nvv
        st = r["nst"].reshape(128, 8, 2, 2)
        for s in range(2):
            sf = st[:, :, s, 0].T.reshape(1024)
            sb = st[:, :, s, 1].T.reshape(1024)
            if rev:
                new_st[2 * core + s, 0, 0], new_st[2 * core + s, 0, 1] = sb, sf
            else:
                new_st[2 * core + s, 0, 0], new_st[2 * core + s, 0, 1] = sf, sb
    return (y_prompt, y_sample, new_k, new_v, new_st)
```

```python
import contextlib
import numpy as np
import ml_dtypes
import concourse.bass as bass
import concourse.mybir as mybir
from concourse.bass_utils import run_bass_kernel_spmd

F32 = mybir.dt.float32
BF16 = mybir.dt.bfloat16
AF = mybir.ActivationFunctionType
ALU = mybir.AluOpType
AX = mybir.AxisListType

NCORES = 8
T_S = 4096
T_OWN = 2048
T_P = 512
EPS = 1e-6
LAM_INIT = 0.2
ENGINES = ("sync", "scalar", "vector", "gpsimd", "tensor")
STOP_AT = None
import os
PBMOD = int(os.environ.get('PBMOD', '4'))
EXPT = os.environ.get('EXPT', '')


class Op:
    __slots__ = ("eng", "fn", "deps", "idx", "dma_key", "signal", "ordinal", "is_dma")

    def __init__(self, eng, fn, idx, dma_key):
        self.eng = eng
        self.fn = fn
        self.idx = idx
        self.dma_key = dma_key
        self.is_dma = dma_key is not None
        self.deps = []
        self.signal = False
        self.ordinal = 0


class Prog:
    def __init__(self, nc):
        self.nc = nc
        self.ops = []
        self.last_w = {}
        self.readers = {}
        self.fence = []
        self.marks = []
        self.cut = None

    def add(self, eng, fn, reads=(), writes=(), dma_key=None):
        op = Op(eng, fn, len(self.ops), dma_key)
        deps = set(self.fence)
        for r in reads:
            w = self.last_w.get(r)
            if w is not None:
                deps.add(w)
            if eng != "tensor" and isinstance(r, tuple) and r[0] == "ps":
                for rd in self.readers.get(r, ()):
                    if rd.eng != eng:
                        deps.add(rd)
        for r in writes:
            w = self.last_w.get(r)
            if w is not None:
                deps.add(w)
            for rd in self.readers.get(r, ()):
                deps.add(rd)
        for r in reads:
            self.readers.setdefault(r, []).append(op)
        for r in writes:
            self.last_w[r] = op
            self.readers[r] = []
        best = {}
        for d in deps:
            if d.eng == "tensor" and eng == "tensor" and not d.is_dma:
                continue
            k = ("dma", d.dma_key) if d.is_dma else ("eng", d.eng)
            if k not in best or best[k].idx < d.idx:
                best[k] = d
        op.deps = list(best.values())
        self.ops.append(op)
        return op

    def set_fence(self):
        last = {}
        for op in self.ops:
            k = ("dma", op.dma_key) if op.is_dma else ("eng", op.eng)
            last[k] = op
        self.fence = list(last.values())
        self.marks.append(len(self.ops))

    def emit(self):
        nc = self.nc
        if self.cut is not None:
            self.ops = self.ops[:self.cut]
        for op in self.ops:
            for d in op.deps:
                d.signal = True
        dma_keys = []
        seen = set()
        dma_owner = {}
        for op in self.ops:
            if op.is_dma:
                op.signal = True
                dma_owner[op.dma_key] = op.eng
                if op.dma_key not in seen:
                    seen.add(op.dma_key)
                    dma_keys.append(op.dma_key)
        cnt = {}
        for op in self.ops:
            if op.is_dma:
                k = ("dma", op.dma_key)
                cnt[k] = cnt.get(k, 0) + 16
                op.ordinal = cnt[k]
            elif op.signal:
                k = ("eng", op.eng)
                cnt[k] = cnt.get(k, 0) + 1
                op.ordinal = cnt[k]
        with contextlib.ExitStack() as st:
            sems = {}
            for e in ENGINES:
                sems[("eng", e)] = st.enter_context(nc.semaphore("s_" + e))
            for i, k in enumerate(dma_keys):
                sems[("dma", k)] = st.enter_context(nc.semaphore("d%d" % i))
            block = st.enter_context(nc.Block())
            per_eng = {e: [] for e in ENGINES}
            for op in self.ops:
                per_eng[op.eng].append(op)

            def body(ename):
                def run(eng):
                    waited = {}
                    for op in per_eng[ename]:
                        for d in op.deps:
                            k = ("dma", d.dma_key) if d.is_dma else ("eng", d.eng)
                            if waited.get(k, 0) >= d.ordinal:
                                continue
                            waited[k] = d.ordinal
                            eng.wait_ge(sems[k], d.ordinal)
                        ins = op.fn(eng)
                        if op.signal:
                            k = ("dma", op.dma_key) if op.is_dma else ("eng", op.eng)
                            ins.then_inc(sems[k], 16 if op.is_dma else 1)
                    for k, v in cnt.items():
                        if k[0] == "dma" and dma_owner.get(k[1]) == ename and waited.get(k, 0) < v:
                            eng.wait_ge(sems[k], v)
                return run

            block.sync(body("sync"))
            block.scalar(body("scalar"))
            block.vector(body("vector"))
            block.gpsimd(body("gpsimd"))
            block.tensor(body("tensor"))


class Arena:
    def __init__(self, nc, nbytes):
        self.ap = nc.alloc_sbuf_tensor("arena", [128, nbytes // 4], F32).ap()
        self.off = 0
        self.nbytes = nbytes

    def alloc(self, free_shape, dtype):
        isz = 4 if dtype == F32 else 2
        n = int(np.prod(free_shape))
        nb = (n * isz + 31) // 32 * 32
        assert self.off + nb <= self.nbytes, ("SBUF arena overflow", self.off, nb)
        v = self.ap[:, self.off // 4:(self.off + nb) // 4]
        self.off += nb
        if dtype != F32:
            v = v.bitcast(dtype)
        v = v[:, 0:n]
        if len(free_shape) == 2:
            v = v.rearrange("p (a b) -> p a b", a=free_shape[0])
        elif len(free_shape) == 3:
            v = v.rearrange("p (a b c) -> p a b c", a=free_shape[0], b=free_shape[1])
        return v


def bcast_last(ap2, n):
    return bass.AP(ap2.tensor, ap2.offset, list(ap2.ap) + [[0, n]])


def bcast_mid(ap2, n):
    a = list(ap2.ap)
    return bass.AP(ap2.tensor, ap2.offset, [a[0], [0, n]] + a[1:])


def build_program():
    nc = bass.Bass("TRN2", target_bir_lowering=False)
    P = Prog(nc)
    notes = []

    def note(nm):
        notes.append((nm, len(P.ops)))

    def din(name, shape, dt=F32):
        return nc.dram_tensor(name, list(shape), dt, kind="ExternalInput").ap()

    def dout(name, shape):
        return nc.dram_tensor(name, list(shape), F32, kind="ExternalOutput").ap()

    xs = din("xs", [T_S, 1024])
    xp = din("xp", [T_P, 1024])
    ck = din("ck", [512, 1024])
    cv = din("cv", [512, 1024])
    cs_d = din("cs", [128, 16])
    w_ada = din("w_ada", [1024, 3072])
    bada_ss = din("bada_ss", [128, 16])
    bada_g = din("bada_g", [128, 1024])
    gpre_c = din("gpre_c", [128, 8])
    gpost_r = din("gpost_r", [128, 1024])
    w_att = din("w_att", [8, 1024, 512])
    pmat_d = din("pmat", [128, 128])
    rope_cos = din("rope_cos", [128, T_S])
    rope_ssin = din("rope_ssin", [128, T_S])
    w_lru = din("w_lru", [8, 1024, 256])
    w_gate = din("w_gate", [128, 8 * 4 * 128])
    lru_vec = din("lru_vec", [128, 8 * 14])
    w_out = din("w_out", [2048, 1024])
    lam_v = din("lam_v", [128, 256])
    gsub_r = din("gsub_r", [128, 128])

    y_s = dout("y_s", [T_OWN, 1024])
    y_p = dout("y_p", [T_P, 1024])
    nk = dout("nk", [T_P, 1024])
    nv = dout("nv", [T_P, 1024])
    nst = dout("nst", [128, 32])

    NTOK = T_OWN + T_P
    mix_d = nc.dram_tensor("mix_d", [2048, NTOK], BF16, kind="Internal").ap()
    gs_d = nc.dram_tensor("gs_d", [128, 2048], F32, kind="Internal").ap()

    ar = Arena(nc, 212736)
    psum = nc.alloc_psum_tensor("psum", [128, 8, 512], F32).ap()

    def bank(i):
        return psum[:, i, :]

    def bank_bf(i):
        return psum[:, i, :].bitcast(BF16)

    def mm(out, lhsT, rhs, start, stop, reads, writes, skip=False):
        P.add("tensor", lambda e: e.matmul(out, lhsT=lhsT, rhs=rhs, start=start, stop=stop,
                                           skip_group_check=skip), reads, writes)

    def tr(out, in_, reads, writes):
        P.add("tensor", lambda e: e.transpose(out=out, in_=in_, identity=ident), list(reads) + ["ident"], writes)

    def act(out, in_, func, reads, writes, scale=1.0, bias=0.0, accum=None):
        if accum is None:
            P.add("scalar", lambda e: e.activation(out=out, in_=in_, func=func, scale=scale, bias=bias), reads, writes)
        else:
            P.add("scalar", lambda e: e.activation(out=out, in_=in_, func=func, scale=scale, bias=bias,
                                                   accum_out=accum), reads, writes)

    def veng(eng, name, reads, writes, *args, **kw):
        P.add(eng, lambda e: getattr(e, name)(*args, **kw), reads, writes)

    def dve(name, reads, writes, *args, **kw):
        veng("vector", name, reads, writes, *args, **kw)

    def pool(name, reads, writes, *args, **kw):
        veng("gpsimd", name, reads, writes, *args, **kw)

    def dma(eng, out, in_, reads, writes, key):
        if eng == "gpsimd" and out.dtype != in_.dtype:
            P.add(eng, lambda e: e.dma_start(out=out, in_=in_, max_dma_last_dim=4096), reads, writes, dma_key=key)
        else:
            P.add(eng, lambda e: e.dma_start(out=out, in_=in_), reads, writes, dma_key=key)

    ident = ar.alloc([128], BF16)
    pmat = ar.alloc([128], BF16)
    hT_off = ar.off
    hT = ar.alloc([8, T_S + T_P], BF16)
    modAB = ar.alloc([16, 2], F32)
    nlam = ar.alloc([2], F32)
    gs = ar.alloc([128], F32)
    lvec = ar.alloc([8, 14], F32)
    lcon = ar.alloc([8, 8], F32)
    st_out = ar.alloc([8, 4], F32)
    small = ar.alloc([64], F32)
    zero_c = ar.alloc([1], F32)
    xbase = ar.off
    wA = [ar.alloc([8, 512], BF16) for _ in range(2)]
    cosT = ar.alloc([T_S], F32)
    sinT = ar.alloc([T_S], F32)
    cbase = ar.off

    pool("memset", [], ["ident"], ident, 0.0)
    P.add("gpsimd", lambda e: e.affine_select(out=ident, in_=ident, pattern=[[-1, 128]], compare_op=ALU.not_equal,
                                              fill=1.0, base=0, channel_multiplier=1), ["ident"], ["ident"])
    pool("memset", [], ["zero_c"], zero_c, 0.0)
    watt_v = [w_att[h].rearrange("(kc p) n -> p kc n", p=128) for h in range(8)]

    def load_head_w(h):
        dma("gpsimd", wA[h % 2], watt_v[h], [], [("wA", h % 2)], ("wA", h % 2))

    load_head_w(0)
    dma("gpsimd", pmat, pmat_d, [], ["pmat"], "pmat")

    cs_t = ar.alloc([8, 2], F32)
    Gs = ar.alloc([2, 1024], F32)
    cs_a = ar.alloc([8, 2], F32)
    csb = ar.alloc([8, 2, 128], BF16)
    cs_b = ar.alloc([8, 2], BF16)
    bss = ar.alloc([16], F32)
    gpre_t = ar.alloc([8], F32)
    bg_t = ar.alloc([1024], F32)
    gpo_t = ar.alloc([1024], F32)
    lam_t = ar.alloc([4, 64], F32)
    lam_p = ar.alloc([2, 64], F32)
    wa = [ar.alloc([8, 256], BF16) for _ in range(4)]

    dma("sync", cs_t.rearrange("p a b -> p (a b)"), cs_d, [], ["cs_t"], "cs_t")
    dma("sync", bss, bada_ss, [], ["bss"], "bss")
    dma("sync", gpre_t, gpre_c, [], ["gpre_t"], "gpre_t")
    dma("sync", bg_t, bada_g, [], ["bg_t"], "bg_t")
    dma("sync", gpo_t, gpost_r, [], ["gpo_t"], "gpo_t")
    dma("sync", lam_t.rearrange("p a b -> p (a b)"), lam_v, [], ["lam_t"], "lam_t")
    dma("sync", gs, gsub_r, [], ["gs"], "gs")
    dma("sync", lvec.rearrange("p a b -> p (a b)"), lru_vec, [], ["lvec"], "lvec")

    act(cs_a, cs_t, AF.Tanh, ["cs_t"], ["cs_a"], scale=0.5)
    dve("scalar_tensor_tensor", ["cs_a", "cs_t"], ["cs_a"], cs_a, cs_a, 1.0, cs_t, ALU.add, ALU.mult)
    dve("tensor_scalar", ["cs_a"], ["cs_a"], cs_a, cs_a, 0.5, None, ALU.mult)
    dve("tensor_copy", ["cs_a"], ["cs_b"], cs_b, cs_a)
    dve("tensor_copy", ["cs_a"], ["csb"], csb, bcast_last(cs_a.rearrange("p a b -> p (a b)"), 128).rearrange(
        "p (a b) c -> p a b c", a=8))
    wada_v = w_ada.rearrange("(kc p) n -> p kc n", p=128)
    def mod_piece(pi):
        slot = pi % 4
        dma("gpsimd", wa[slot], wada_v[:, :, pi * 256:(pi + 1) * 256], [], [("wa", slot)], ("wa", slot))
        if pi < 8:
            for o2 in range(2):
                oc = pi * 2 + o2
                for kc in range(8):
                    mm(bank(7)[:, oc * 2:oc * 2 + 2], wa[slot][:, kc, o2 * 128:(o2 + 1) * 128], cs_b[:, kc, :],
                       kc == 0, kc == 7, [("wa", slot), "cs_b"], [("ps", 7)])
        else:
            q_ = pi - 8
            for j in range(2):
                for kc in range(8):
                    mm(bank(j)[:, 0:256], csb[:, kc, j, :], wa[slot][:, kc, :], kc == 0, kc == 7, [("wa", slot), "csb"], [("ps", j)])
                gsl = Gs[:, j, q_ * 256:(q_ + 1) * 256]
                dve("tensor_tensor", [("ps", j), "bg_t"], ["Gs"], gsl, bank(j)[:, 0:256], bg_t[:, q_ * 256:(q_ + 1) * 256], ALU.add)
                dve("tensor_tensor", ["Gs", "gpo_t"], ["Gs"], gsl, gsl, gpo_t[:, q_ * 256:(q_ + 1) * 256], ALU.mult)
        if pi == 7:
            dve("tensor_tensor", [("ps", 7), "bss"], ["modAB"], modAB, bank(7)[:, 0:32].rearrange("p (a b) -> p a b", b=2),
                bcast_last(bss, 2), ALU.add)
            dve("tensor_scalar", ["modAB"], ["modAB"], modAB[:, 8:16, :], modAB[:, 8:16, :], 1.0, None, ALU.add)
            dve("tensor_tensor", ["modAB", "gpre_t"], ["modAB"], modAB[:, 8:16, :], modAB[:, 8:16, :],
                bcast_last(gpre_t, 2), ALU.mult)
    for pi in range(8):
        mod_piece(pi)
    dve("tensor_tensor", ["lam_t"], ["lam_p"], lam_p, lam_t[:, 0:2, :], lam_t[:, 2:4, :], ALU.mult)
    dve("tensor_reduce", ["lam_p"], ["small"], small[:, 0:2], lam_p, AX.X, ALU.add)
    act(small[:, 2:4], small[:, 0:2], AF.Exp, ["small"], ["small"])
    dve("tensor_tensor", ["small"], ["small"], small[:, 4:5], small[:, 2:3], small[:, 3:4], ALU.subtract)
    dve("tensor_scalar", ["small"], ["nlam"], nlam[:, 0:1], small[:, 4:5], LAM_INIT, -1.0, ALU.add, ALU.mult)
    dve("tensor_scalar", ["gs"], ["gs"], gs, gs, (1.0 - LAM_INIT) * 0.5, None, ALU.mult)
    dve("tensor_scalar", ["lvec"], ["lcon"], lcon[:, :, 0:4], lvec[:, :, 6:10], 0.5, None, ALU.mult)
    lsp = ar.alloc([8, 2], F32)
    act(lsp, lvec[:, :, 10:12], AF.Exp, ["lvec"], ["lsp"], scale=-1.0)
    act(lsp, lsp, AF.Ln, ["lsp"], ["lsp"], bias=1.0)
    dve("tensor_scalar", ["lsp"], ["lcon"], lcon[:, :, 4:6], lsp, -4.0, None, ALU.mult)

    dma("gpsimd", cosT, rope_cos, [], ["cosT"], "cosT")
    dma("gpsimd", sinT, rope_ssin, [], ["sinT"], "sinT")

    xt = [ar.alloc([1024], F32) for _ in range(6)]
    xn = [ar.alloc([1024], BF16) for _ in range(4)]
    ssq = ar.alloc([40], F32)
    rstd = ar.alloc([40], F32)
    def b_front(g):
        pb = (g % 2) * 4
        for i in range(4):
            ti = g * 4 + i
            src = xs[ti * 128:(ti + 1) * 128, :] if g < 8 else xp[i * 128:(i + 1) * 128, :]
            dma("sync", xt[ti % 6], src, [], [("xt", ti % 6)], ("xt", ti % 6))
            act(xn[i], xt[ti % 6], AF.Square, [("xt", ti % 6)], [("xn", i), ("ssq", ti)], accum=ssq[:, ti:ti + 1])
            act(ssq[:, ti:ti + 1], ssq[:, ti:ti + 1], AF.Sqrt, [("ssq", ti)], [("ssq", ti)], scale=1.0 / 1024, bias=EPS)
            dve("reciprocal", [("ssq", ti)], [("rstd", ti)], rstd[:, ti:ti + 1], ssq[:, ti:ti + 1])
            dve("tensor_scalar", [("xt", ti % 6), ("rstd", ti)], [("xn", i)], xn[i], xt[ti % 6], rstd[:, ti:ti + 1], None, ALU.mult)
            for kc in range(8):
                b_ = pb + kc // 2
                tr(bank_bf(b_)[:, (kc % 2) * 512 + i * 128:(kc % 2) * 512 + (i + 1) * 128], xn[i][:, kc * 128:(kc + 1) * 128],
                   [("xn", i)], [("ps", b_)])

    def b_back(g):
        j = 0 if g < 8 else 1
        pb = (g % 2) * 4
        for kc in range(8):
            b_ = pb + kc // 2
            src = bank_bf(b_)[:, (kc % 2) * 512:(kc % 2) * 512 + 512]
            dst = hT[:, kc, g * 512:(g + 1) * 512]
            if (kc // 2) != 1:
                dve("tensor_scalar", [("ps", b_), "modAB"], [("hT", g)], dst, src, modAB[:, 8 + kc, j:j + 1],
                    modAB[:, kc, j:j + 1], ALU.mult, ALU.add)
            else:
                act(dst, src, AF.Identity, [("ps", b_), "modAB"], [("hT", g)], scale=modAB[:, 8 + kc, j:j + 1],
                    bias=modAB[:, kc, j:j + 1])

    b_front(0)
    for g in range(1, 9):
        b_front(g)
        b_back(g - 1)
    b_back(8)
    for pi in range(8, 12):
        mod_piece(pi)
    dma("sync", gs_d, Gs.rearrange("p a b -> p (a b)"), ["Gs"], [], "gs_d")
    P.set_fence()

    ar.off = cbase
    kT = ar.alloc([512 + T_S], BF16)
    qT = ar.alloc([T_OWN], BF16)
    vh = ar.alloc([36, 130], BF16)
    ckb = ar.alloc([4, 128], BF16)
    kTp = ar.alloc([T_P], BF16)
    qTp = ar.alloc([T_P], BF16)
    vp = ar.alloc([4, 130], BF16)
    gatt = ar.alloc([20, 128], F32)
    nk_st = ar.alloc([4, 128], F32)
    nv_st = ar.alloc([4, 128], F32)
    Et = [ar.alloc([2, 512], BF16) for _ in range(3)]
    rt1 = [ar.alloc([512], F32) for _ in range(2)]
    qb = [ar.alloc([512], BF16) for _ in range(2)]
    rt2 = [ar.alloc([512], F32) for _ in range(2)]
    o_all = ar.alloc([20, 128], F32)
    o1 = ar.alloc([4, 128], F32)
    rden = ar.alloc([8], F32)
    avs = ar.alloc([8, 129], F32)
    ssj = ar.alloc([3, 4], F32)
    att_b = ar.alloc([12, 128], BF16)
    mixo = [ar.alloc([512], BF16) for _ in range(3)]

    pool("memset", [], ["vh1"], vh[:, :, 128:130], 1.0)
    pool("memset", [], ["vp1"], vp[:, :, 128:130], 1.0)

    ck_v = ck.rearrange("(t p) f -> p t f", p=128)
    cv_v = cv.rearrange("(t p) f -> p t f", p=128)
    nk_v = nk.rearrange("(t p) f -> p t f", p=128)
    nv_v = nv.rearrange("(t p) f -> p t f", p=128)

    ecount = [0]
    scount = [0]

    def attention_jobs(jobs, h):
        items = []
        for ji, jb in enumerate(jobs):
            for kt in range(len(jb[0])):
                items.append((ji, kt))

        def region(r):
            return bank(4 + r // 3)[:, (r % 3) * 129:(r % 3) * 129 + 129]

        def issue_S(idx):
            ji, kt = items[idx]
            kT_tiles, v_tiles, q_ap, nq, o_tile0, rd_extra = jobs[ji]
            ss_ = idx % 2
            es = idx % 3
            ka, kkey = kT_tiles[kt]
            for m in range(2):
                mm(bank(2 * ss_ + m)[:, 0:nq], ka[m * 64:(m + 1) * 64, :], q_ap[m * 64:(m + 1) * 64, :], True, True,
                   [kkey] + rd_extra, [("ps", 2 * ss_ + m)])
            if nq == 512:
                act(Et[es].rearrange("p a b -> p (a b)"), psum[:, 2 * ss_:2 * ss_ + 2, :].rearrange("p a b -> p (a b)"), AF.Exp,
                    [("ps", 2 * ss_), ("ps", 2 * ss_ + 1)], [("E", es)], scale=0.125)
            else:
                act(Et[es][:, :, 0:nq], psum[:, 2 * ss_:2 * ss_ + 2, 0:nq], AF.Exp, [("ps", 2 * ss_), ("ps", 2 * ss_ + 1)], [("E", es)], scale=0.125)

        def issue_AV(idx):
            ji, kt = items[idx]
            kT_tiles, v_tiles, q_ap, nq, o_tile0, rd_extra = jobs[ji]
            nqs = nq // 128
            nkt = len(kT_tiles)
            es = idx % 3
            va, vkey = v_tiles[kt]
            for m in range(2):
                for qs in range(nqs):
                    r = m * nqs + qs
                    mm(region(r), Et[es][:, m, qs * 128:(qs + 1) * 128], va, (kt == 0 and r % 3 == 0), kt == nkt - 1,
                       [("E", es), vkey], [("ps", 4 + r // 3)], skip=True)
            if kt == nkt - 1:
                finish(ji)

        def finish(ji):
            kT_tiles, v_tiles, q_ap, nq, o_tile0, rd_extra = jobs[ji]
            nqs = nq // 128
            nreg = 2 * nqs
            nb = (nreg + 2) // 3
            for b_ in range(nb):
                n_in = min(3, nreg - 3 * b_)
                dve("tensor_copy", [("ps", 4 + b_)], ["avs"], avs[:, 3 * b_:3 * b_ + n_in, :].rearrange("p a b -> p (a b)"),
                    bank(4 + b_)[:, 0:n_in * 129])
            dve("reciprocal", ["avs"], ["rden"], rden[:, 0:nreg], avs[:, 0:nreg, 128])
            dve("tensor_scalar", ["rden", "nlam"], ["rden"], rden[:, nqs:nreg], rden[:, nqs:nreg], nlam[:, 0:1], None, ALU.mult)
            for qs in range(nqs):
                r1, r2 = qs, nqs + qs
                dve("tensor_scalar", ["avs", "rden"], ["o1"], o1[:, qs, :], avs[:, r1, 0:128], rden[:, r1:r1 + 1], None, ALU.mult)
                dve("scalar_tensor_tensor", ["avs", "rden", "o1"], [("o", o_tile0 + qs)], o_all[:, o_tile0 + qs, :],
                    avs[:, r2, 0:128], rden[:, r2:r2 + 1], o1[:, qs, :], ALU.mult, ALU.add)
            sl = ji % 3
            okk = [("o", o_tile0 + qs) for qs in range(nqs)]
            dve("tensor_tensor", okk + ["o1"], ["o1"], o1[:, 0:nqs, :], o_all[:, o_tile0:o_tile0 + nqs, :], o_all[:, o_tile0:o_tile0 + nqs, :], ALU.mult)
            dve("tensor_reduce", ["o1"], [("ssj", sl)], ssj[:, sl, 0:nqs], o1[:, 0:nqs, :], AX.X, ALU.add)

            def part2():
                act(ssj[:, sl, 0:nqs], ssj[:, sl, 0:nqs], AF.Ln, [("ssj", sl)], [("ssj", sl)], scale=1.0 / 128, bias=EPS)
                act(ssj[:, sl, 0:nqs], ssj[:, sl, 0:nqs], AF.Exp, [("ssj", sl)], [("ssj", sl)], scale=-0.5)

            def part3():
                for qs in range(nqs):
                    t_ = o_tile0 + qs
                    dve("scalar_tensor_tensor", [("o", t_), ("ssj", sl), ("gatt", t_ // 4)], [("att", sl)], att_b[:, sl * 4 + qs, :], o_all[:, t_, :],
                        ssj[:, sl, qs:qs + 1], gatt[:, t_, :], ALU.mult, ALU.mult)
                for qs in range(nqs):
                    tr(bank_bf(7)[:, qs * 128:(qs + 1) * 128], att_b[:, sl * 4 + qs, :], [("att", sl)], [("ps", 7)])
                dve("tensor_copy", [("ps", 7)], [("mixo", sl)], mixo[sl][:, 0:nq], bank_bf(7)[:, 0:nq])
                dma("sync", mix_d[h * 128:(h + 1) * 128, o_tile0 * 128:o_tile0 * 128 + nq], mixo[sl][:, 0:nq], [("mixo", sl)], [], ("mixo", sl))

            pending.append([cur[0] + 8, part2])
            pending.append([cur[0] + 11, part3])

        pending = []
        cur = [0]
        n = len(items)
        issue_S(0)
        if n > 1:
            issue_S(1)
        for idx in range(n):
            cur[0] = idx
            if idx + 2 < n:
                issue_S(idx + 2)
            issue_AV(idx)
            while pending and pending[0][0] <= idx:
                pending.pop(0)[1]()
        return pending

    pend_prev = [[]]
    for h in range(8):
        w = wA[h % 2]
        wk = ("wA", h % 2)
        if h + 1 < 8:
            load_head_w(h + 1)
        dma("gpsimd", ckb, ck_v[:, :, h * 128:(h + 1) * 128], [], ["ckb"], "ckb")
        dma("gpsimd", vh[:, 0:4, 0:128], cv_v[:, :, h * 128:(h + 1) * 128], [], [("vh", "c")], ("vh", "c"))
        for t in range(4):
            tr(bank_bf(7)[:, t * 128:(t + 1) * 128], ckb[:, t, :], ["ckb"], [("ps", 7)])
        dve("tensor_copy", [("ps", 7)], [("kT", "c")], kT[:, 0:512], bank_bf(7)[:, 0:512])
        note('h%d rope' % h)
        def rope_finish(gi):
            isq = gi >= 8
            g = gi - 8 if isq else gi
            pb = (gi % 2) * 2
            s_ = gi % 2
            mm(bank(pb + 1), pmat, qb[s_], True, True, [("qb", s_), "pmat"], [("ps", pb + 1)])
            dve("tensor_tensor", [("ps", pb + 1), "sinT"], [("rt2", s_)], rt2[s_], bank(pb + 1), sinT[:, g * 512:(g + 1) * 512], ALU.mult)
            if isq:
                pool("tensor_tensor", [("rt1", s_), ("rt2", s_)], [("qT", g)], qT[:, g * 512:(g + 1) * 512], rt1[s_], rt2[s_], ALU.add)
            else:
                pool("tensor_tensor", [("rt1", s_), ("rt2", s_)], [("kT", g)], kT[:, 512 + g * 512:512 + (g + 1) * 512], rt1[s_], rt2[s_], ALU.add)

        for gi in range(12):
            isq = gi >= 8
            g = gi - 8 if isq else gi
            c0 = 0 if isq else 128
            pb = (gi % 2) * 2
            s_ = gi % 2
            for kc in range(8):
                mm(bank(pb), w[:, kc, c0:c0 + 128], hT[:, kc, g * 512:(g + 1) * 512], kc == 0, kc == 7, [wk, ("hT", g)], [("ps", pb)])
            if gi >= 1:
                rope_finish(gi - 1)
            act(qb[s_], bank(pb), AF.Copy, [("ps", pb)], [("qb", s_)])
            dve("tensor_tensor", [("ps", pb), "cosT"], [("rt1", s_)], rt1[s_], bank(pb), cosT[:, g * 512:(g + 1) * 512], ALU.mult)
        rope_finish(11)
        while pend_prev[0]:
            pend_prev[0].pop(0)[1]()
        note('h%d v' % h)
        for i0 in range(0, 16, 2):
            pb = (i0 // 2) % PBMOD
            for ii in range(2):
                i = i0 + ii
                for kc in range(8):
                    mm(bank(pb)[:, ii * 256:(ii + 1) * 256], hT[:, kc, i * 128:(i + 1) * 128], w[:, kc, 256:512], kc == 0 and ii == 0, kc == 7,
                       [wk, ("hT", i // 4)], [("ps", pb)], skip=True)
            pv = bank(pb).rearrange("p (a b) -> p a b", a=2)
            dve("tensor_copy", [("ps", pb)], [("vh", (4 + i0) // 4), "vhx"], vh[:, 4 + i0:6 + i0, 0:128], pv[:, :, 0:128])
            act(gatt[:, i0:i0 + 2, :], pv[:, :, 128:256], AF.Tanh if EXPT != 'B' else AF.Identity, [("ps", pb)] + (["vhx"] if EXPT == 'A' else []), [("gatt", i0 // 4)], scale=0.5)
            dve("scalar_tensor_tensor", [("gatt", i0 // 4), ("ps", pb)], [("gatt", i0 // 4)], gatt[:, i0:i0 + 2, :], gatt[:, i0:i0 + 2, :], 1.0,
                pv[:, :, 128:256], ALU.add, ALU.mult)
        for i0 in range(16, 32, 4):
            pb = (i0 // 4) % 4
            for ii in range(4):
                i = i0 + ii
                for kc in range(8):
                    mm(bank(pb)[:, ii * 128:(ii + 1) * 128], hT[:, kc, i * 128:(i + 1) * 128], w[:, kc, 256:384], kc == 0 and ii == 0, kc == 7,
                       [wk, ("hT", i // 4)], [("ps", pb)], skip=True)
            act(vh[:, 4 + i0:8 + i0, 0:128], bank(pb).rearrange("p (a b) -> p a b", a=4), AF.Copy, [("ps", pb)], [("vh", (4 + i0) // 4)])
        note('h%d prompt' % h)
        for kc in range(8):
            mm(bank(0), w[:, kc, 0:128], hT[:, kc, T_S:T_S + T_P], kc == 0, kc == 7, [wk, ("hT", 8)], [("ps", 0)])
        for kc in range(8):
            mm(bank(1), w[:, kc, 128:256], hT[:, kc, T_S:T_S + T_P], kc == 0, kc == 7, [wk, ("hT", 8)], [("ps", 1)])
        act(qTp, bank(0), AF.Copy, [("ps", 0)], ["qTp"])
        dve("tensor_copy", [("ps", 1)], ["kTp"], kTp, bank(1))
        for i in range(4):
            pb = 2 + i % 2
            for kc in range(8):
                mm(bank(pb)[:, 0:384], hT[:, kc, T_S + i * 128:T_S + (i + 1) * 128], w[:, kc, 128:512], kc == 0, kc == 7,
                   [wk, ("hT", 8)], [("ps", pb)])
            dve("tensor_copy", [("ps", pb)], ["nk_st"], nk_st[:, i, :], bank(pb)[:, 0:128])
            dve("tensor_copy", [("ps", pb)], ["nv_st"], nv_st[:, i, :], bank(pb)[:, 128:256])
            act(vp[:, i, 0:128], bank(pb)[:, 128:256], AF.Copy, [("ps", pb)], [("vp", i // 2)])
            act(gatt[:, 16 + i, :], bank(pb)[:, 256:384], AF.Tanh, [("ps", pb)], [("gatt", 4)], scale=0.5)
            dve("scalar_tensor_tensor", [("gatt", 4), ("ps", pb)], [("gatt", 4)], gatt[:, 16 + i, :], gatt[:, 16 + i, :], 1.0,
                bank(pb)[:, 256:384], ALU.add, ALU.mult)
        dma("gpsimd", nk_v[:, :, h * 128:(h + 1) * 128], nk_st, ["nk_st"], [], "nk_out")
        dma("gpsimd", nv_v[:, :, h * 128:(h + 1) * 128], nv_st, ["nv_st"], [], "nv_out")
        note('h%d attn' % h)
        kt_list = [(kT[:, t * 128:(t + 1) * 128], ("kT", "c")) for t in range(4)] + \
                  [(kT[:, 512 + i * 128:512 + (i + 1) * 128], ("kT", i // 4)) for i in range(32)]
        v_list = [(vh[:, t, 0:129], ("vh", "c")) for t in range(4)] + [(vh[:, 4 + i, 0:129], ("vh", (4 + i) // 4)) for i in range(32)]
        jobs = []
        for qg in range(4):
            jobs.append((kt_list, v_list, qT[:, qg * 512:(qg + 1) * 512], 512, qg * 4, [("qT", qg), "vh1"]))
        for s in range(2):
            ktl = [(kTp[:, s * 256 + t * 128:s * 256 + (t + 1) * 128], "kTp") for t in range(2)]
            vl = [(vp[:, s * 2 + t, 0:129], ("vp", s)) for t in range(2)]
            jobs.append((ktl, vl, qTp[:, s * 256:(s + 1) * 256], 256, 16 + s * 2, ["qTp", "vp1"]))
        gk = [("gatt", i) for i in range(5)]
        pool("tensor_tensor", gk + ["gs"], gk, gatt, gatt, bcast_mid(gs, 20), ALU.mult)
        pend_prev[0] = attention_jobs(jobs, h)
        note('h%d subln' % h)
    while pend_prev[0]:
        pend_prev[0].pop(0)[1]()
    P.set_fence()

    ar.off = xbase
    wL = [ar.alloc([8, 256], BF16) for _ in range(2)]
    wG = ar.alloc([8, 4, 128], BF16)
    xl = ar.alloc([T_S + 4 + 2 * 260], BF16)
    sgl = ar.alloc([NTOK], F32)
    NBT = 1024
    ubuf = [ar.alloc([NBT], F32) for _ in range(2)] + [ar.alloc([256], F32) for _ in range(2)]
    ubb = [ar.alloc([NBT], BF16) for _ in range(2)] + [ar.alloc([256], BF16) for _ in range(2)]
    Abk = [ar.alloc([NBT], F32) for _ in range(2)] + [ar.alloc([256], F32) for _ in range(2)]
    Ibk = [ar.alloc([NBT], F32) for _ in range(2)] + [ar.alloc([256], F32) for _ in range(2)]
    Mbk = [ar.alloc([NBT], F32) for _ in range(2)] + [ar.alloc([256], F32) for _ in range(2)]
    Afw = ar.alloc([T_OWN], F32)
    Ifw = ar.alloc([T_OWN], F32)
    Mfw = ar.alloc([T_OWN], F32)
    hbs = ar.alloc([NTOK], F32)
    hfs = ar.alloc([NTOK], F32)
    wD1 = ar.alloc([5, 128], BF16)
    wD = [wD1, wD1]
    lru_b = ar.alloc([NTOK], BF16)

    dma("gpsimd", wG.rearrange("p a b c -> p (a b c)"), w_gate, [], ["wG"], "wG")
    pool("memset", [], [("xl", g) for g in range(9)], xl, 0.0)
    wlru_v = [w_lru[j].rearrange("(kc p) n -> p kc n", p=128) for j in range(8)]
    dve("tensor_scalar", ["lvec"], ["lcon"], lcon[:, :, 6:8], lvec[:, :, 12:14], 2.0, None, ALU.mult)

    def load_lru_w(j):
        dma("gpsimd", wL[j % 2], wlru_v[j], [], [("wL", j % 2)], ("wL", j % 2))

    gcount = [0]
    ccount = [0]

    class Batch:
        def __init__(self, st_, xoff, n, fwd_off):
            self.set = st_
            self.xoff = xoff
            self.n = n
            self.L = 512 if n >= 512 else n
            self.nch = n // self.L
            self.fwd_off = fwd_off

        def kb(self, nm, c):
            return (nm, self.set, c)

        def kf(self, nm, c):
            return (nm, (self.fwd_off + c * self.L) // 512)

        def allk(self, nm, fwd):
            return [self.kf(nm, c) if fwd else self.kb(nm, c) for c in range(self.nch)]

    def xl_keys(x0, L):
        if x0 >= 4100:
            return [("xl", 8)]
        g0 = max(0, (x0 - 2) // 512)
        g1 = min(7, (x0 + L + 1) // 512)
        return [("xl", g) for g in range(g0, g1 + 1)]

    def stage_A1(j, B):
        L = B.L
        for c in range(B.nch):
            u = ubuf[B.set][:, c * L:(c + 1) * L]
            x0 = B.xoff + c * L
            uk = B.kb("u", c)
            cb_ = 4 + ccount[0] % 4
            ccount[0] += 1
            for tp in range(5):
                mm(bank(cb_)[:, 0:L], wD[j % 2][:, tp, :], xl[:, x0 + tp:x0 + tp + L], tp == 0, tp == 4, xl_keys(x0, L) + ["wD"], [("ps", cb_)])
            dve("tensor_scalar", [("ps", cb_), "lvec"], [uk], u, bank(cb_)[:, 0:L], lvec[:, j, 5:6], None, ALU.add)
            dve("tensor_copy", [uk], [B.kb("ubb", c)], ubb[B.set][:, c * L:(c + 1) * L], u)

    def stage_A2(j, B):
        L = B.L
        for c in range(B.nch):
            u = ubuf[B.set][:, c * L:(c + 1) * L]
            uk = B.kb("u", c)
            ubc = ubb[B.set][:, c * L:(c + 1) * L]
            dirs = [1] if B.fwd_off is None else [0, 1]
            for d in dirs:
                gi = gcount[0] % 2
                gcount[0] += 1
                pr, pi_ = gi * 2, gi * 2 + 1
                mm(bank(pr)[:, 0:L], wG[:, j, 2 * d, :], ubc, True, True, ["wG", B.kb("ubb", c)], [("ps", pr)])
                mm(bank(pi_)[:, 0:L], wG[:, j, 2 * d + 1, :], ubc, True, True, ["wG", B.kb("ubb", c)], [("ps", pi_)])
                if d == 1:
                    A_, I_, M_ = (t_[B.set][:, c * L:(c + 1) * L] for t_ in (Abk, Ibk, Mbk))
                    ka, ki, km = B.kb("A", c), B.kb("I", c), B.kb("M", c)
                else:
                    o = B.fwd_off + c * L
                    A_, I_, M_ = Afw[:, o:o + L], Ifw[:, o:o + L], Mfw[:, o:o + L]
                    ka, ki, km = B.kf("Af", c), B.kf("If", c), B.kf("Mf", c)
                act(A_, bank(pr)[:, 0:L], AF.Tanh, [("ps", pr), "lcon"], [ka], scale=0.5, bias=lcon[:, j, d:d + 1])
                act(A_, A_, AF.Exp, [ka, "lcon"], [ka], scale=lcon[:, j, 4 + d:5 + d], bias=lcon[:, j, 4 + d:5 + d])
                act(I_, bank(pi_)[:, 0:L], AF.Tanh, [("ps", pi_), "lcon"], [ki], scale=0.5, bias=lcon[:, j, 2 + d:3 + d])
                pool("tensor_tensor", [ka], [km], M_, A_, A_, ALU.mult)
                dve("scalar_tensor_tensor", [ki, uk], [ki], I_, I_, 1.0, u, ALU.add, ALU.mult)

    def stage_A(j, B):
        stage_A1(j, B)
        stage_A2(j, B)

    def arrs(B, d):
        if d == 1:
            return (Abk[B.set][:, 0:B.n], Ibk[B.set][:, 0:B.n], Mbk[B.set][:, 0:B.n], B.allk("A", False), B.allk("I", False), B.allk("M", False))
        o = B.fwd_off
        return (Afw[:, o:o + B.n], Ifw[:, o:o + B.n], Mfw[:, o:o + B.n], B.allk("Af", True), B.allk("If", True), B.allk("Mf", True))

    def stage_S(B):
        for d in ([1] if B.fwd_off is None else [0, 1]):
            A_, I_, M_, ka, ki, km = arrs(B, d)
            act(M_, M_, AF.Sqrt, km, km, scale=-1.0, bias=1.0)

    def stage_C(B):
        for d in ([1] if B.fwd_off is None else [0, 1]):
            A_, I_, M_, ka, ki, km = arrs(B, d)
            dve("tensor_tensor", ki + km, ki, I_, I_, M_, ALU.mult)

    def scan_b(B, init_ap, init_key, out_ap, out_key):
        A_, I_, M_, ka, ki, km = arrs(B, 1)
        dve("tensor_tensor_scan", ka + ki + [init_key], [out_key], out_ap[:, ::-1], A_[:, ::-1], I_[:, ::-1], init_ap, ALU.mult, ALU.add)

    _save = ar.off
    ar.off = hT_off
    wO = ar.alloc([16, 1024], BF16)
    ar.off = _save
    wout_v = w_out.rearrange("(c p) n -> p c n", p=128)

    load_lru_w(0)
    for j in range(8):
        w = wL[j % 2]
        wk = ("wL", j % 2)
        if j + 1 < 8:
            load_lru_w(j + 1)
        for tp in range(5):
            dve("tensor_scalar", ["ident", "lvec"], ["wD"], wD[j % 2][:, tp, :], ident, lvec[:, j, tp:tp + 1], None, ALU.mult)
        for g in [7, 6, 5, 4, 3, 2, 1, 0, 8]:
            pb = 4 + g % 4
            for kc in range(8):
                mm(bank(pb), w[:, kc, 0:128], hT[:, kc, g * 512:(g + 1) * 512], kc == 0, kc == 7, [wk, ("hT", g)], [("ps", pb)])
            if g < 8:
                act(xl[:, 2 + g * 512:2 + (g + 1) * 512], bank(pb), AF.Copy, [("ps", pb)], [("xl", g)])
            else:
                for s in range(2):
                    act(xl[:, 4102 + s * 260:4102 + s * 260 + 256], bank(pb)[:, s * 256:(s + 1) * 256], AF.Copy, [("ps", pb)], [("xl", 8)])
        for gi, g in enumerate([0, 1, 2, 3, 8]):
            pb = gi % 4
            for kc in range(8):
                mm(bank(pb), w[:, kc, 128:256], hT[:, kc, g * 512:(g + 1) * 512], kc == 0, kc == 7, [wk, ("hT", g)], [("ps", pb)])
            sg = sgl[:, gi * 512:(gi + 1) * 512]
            act(sg, bank(pb), AF.Tanh, [("ps", pb)], [("sgl", gi)], scale=0.5)
            dve("scalar_tensor_tensor", [("sgl", gi), ("ps", pb)], [("sgl", gi)], sg, sg, 1.0, bank(pb), ALU.add, ALU.mult)
        if j == 7:
            for c4 in range(4):
                dma("gpsimd", wO[:, c4 * 4:(c4 + 1) * 4, :], wout_v[:, c4 * 4:(c4 + 1) * 4, :], [], [("wO", c4)] + [("hT", g) for g in range(9)],
                    ("wO", c4))
        B0 = Batch(0, 3072, 1024, None)
        B1 = Batch(1, 2048, 1024, None)
        B2 = Batch(0, 1024, 1024, 1024)
        B3 = Batch(1, 0, 1024, 0)
        B4 = Batch(2, 4100, 256, 0)
        B5 = Batch(3, 4360, 256, 512)
        carry = small[:, 8:9]
        stage_A1(j, B4)
        stage_A1(j, B5)
        stage_A1(j, B0)
        stage_A1(j, B1)
        stage_A2(j, B4)
        stage_A2(j, B5)
        stage_A2(j, B0)
        stage_A2(j, B1)
        stage_S(B4)
        stage_S(B5)
        stage_S(B0)
        stage_S(B1)
        for s, B in enumerate((B4, B5)):
            stage_C(B)
            p0 = T_OWN + s * 256
            scan_b(B, zero_c[:, 0:1], "zero_c", hbs[:, p0:p0 + 256], ("hbs", 2 + s))
            A_, I_, M_, ka, ki, km = arrs(B, 0)
            dve("tensor_tensor_scan", ka + ki + ["zero_c"], [("hfs", 2 + s)], hfs[:, p0:p0 + 256], A_, I_, zero_c[:, 0:1], ALU.mult, ALU.add)
            dve("tensor_scalar", [("hfs", 2 + s)], ["st_out"], st_out[:, j, 2 * s:2 * s + 1], hfs[:, p0 + 255:p0 + 256], 0.5, None, ALU.mult)
            dve("tensor_scalar", [("hbs", 2 + s)], ["st_out"], st_out[:, j, 2 * s + 1:2 * s + 2], hbs[:, p0:p0 + 1], 0.5, None, ALU.mult)
            pool("tensor_tensor", [("hfs", 2 + s), ("hbs", 2 + s)], [("hfs", 2 + s)], hfs[:, p0:p0 + 256], hfs[:, p0:p0 + 256], hbs[:, p0:p0 + 256], ALU.add)
            dve("scalar_tensor_tensor", [("hfs", 2 + s), ("sgl", 4)], [("lru_b", 1)], lru_b[:, p0:p0 + 256], hfs[:, p0:p0 + 256], 0.25,
                sgl[:, p0:p0 + 256], ALU.mult, ALU.mult)
        dma("sync", mix_d[(8 + j) * 128:(9 + j) * 128, T_OWN:NTOK], lru_b[:, T_OWN:NTOK], [("lru_b", 1)], [], ("lru_b", 1))
        stage_C(B0)
        scan_b(B0, lcon[:, j, 7:8], "lcon", Mbk[0], ("M", 0, 0))
        pool("tensor_copy", [("M", 0, 0)], ["carry"], carry, Mbk[0][:, 0:1])
        stage_A1(j, B2)
        stage_C(B1)
        scan_b(B1, carry, "carry", Mbk[1], ("M", 1, 0))
        pool("tensor_copy", [("M", 1, 0)], ["carry"], carry, Mbk[1][:, 0:1])
        stage_A1(j, B3)
        stage_A2(j, B2)
        stage_A2(j, B3)
        stage_S(B2)
        stage_S(B3)
        stage_C(B2)
        scan_b(B2, carry, "carry", hbs[:, 1024:2048], ("hbs", 1))
        stage_C(B3)
        scan_b(B3, hbs[:, 1024:1025], ("hbs", 1), hbs[:, 0:1024], ("hbs", 0))
        fk = [("Af", c) for c in range(4)] + [("If", c) for c in range(4)]
        dve("tensor_tensor_scan", fk + ["lcon"], [("hfs", 0, "a"), ("hfs", 0, "b")], hfs[:, 0:T_OWN], Afw, Ifw, lcon[:, j, 6:7], ALU.mult, ALU.add)
        pool("tensor_tensor", [("hfs", 0, "b"), ("hbs", 1)], [("hfs", 0, "b")], hfs[:, 1024:T_OWN], hfs[:, 1024:T_OWN], hbs[:, 1024:T_OWN], ALU.add)
        dve("tensor_tensor", [("hfs", 0, "a"), ("hbs", 0)], [("hfs", 0, "a")], hfs[:, 0:1024], hfs[:, 0:1024], hbs[:, 0:1024], ALU.add)
        sk = [("sgl", gi) for gi in range(4)]
        dve("scalar_tensor_tensor", [("hfs", 0, "a")] + sk, [("lru_b", 0)], lru_b[:, 0:1024], hfs[:, 0:1024], 0.25, sgl[:, 0:1024], ALU.mult, ALU.mult)
        dve("scalar_tensor_tensor", [("hfs", 0, "b")] + sk, [("lru_b", 0)], lru_b[:, 1024:T_OWN], hfs[:, 1024:T_OWN], 0.25, sgl[:, 1024:T_OWN], ALU.mult, ALU.mult)
        dma("sync", mix_d[(8 + j) * 128:(9 + j) * 128, 0:T_OWN], lru_b[:, 0:T_OWN], [("lru_b", 0)], [], ("lru_b", 0))
    dma("sync", nst, st_out.rearrange("p a b -> p (a b)"), ["st_out"], [], "nst")
    P.set_fence()

    ar.off = xbase
    mt = [ar.alloc([16, 128], BF16) for _ in range(4)]
    xr = [ar.alloc([1024], F32) for _ in range(4)]
    yt = [ar.alloc([1024], F32) for _ in range(4)]
    junk2 = ar.alloc([1024], BF16)
    ss_e = ar.alloc([20], F32)
    Gs = ar.alloc([2, 1024], F32)
    dma("sync", Gs.rearrange("p a b -> p (a b)"), gs_d, [], ["Gs"], "gs_ld")
    rs_e = ar.alloc([20], F32)
    mixd_v = mix_d.rearrange("(c p) t -> p c t", p=128)
    for i in range(20):
        sl = i % 4
        j = 0 if i < 16 else 1
        dma("sync", mt[sl], mixd_v[:, :, i * 128:(i + 1) * 128], [], [("mt", sl)], ("mt", sl))
        xsrc = xs[i * 128:(i + 1) * 128, :] if i < 16 else xp[(i - 16) * 128:(i - 15) * 128, :]
        dma("sync", xr[sl], xsrc, [], [("xr", sl)], ("xr", sl))
        pb = (i % 4) * 2
        for hc in range(2):
            for c in range(16):
                mm(bank(pb + hc), mt[sl][:, c, :], wO[:, c, hc * 512:(hc + 1) * 512], c == 0, c == 15, [("mt", sl), ("wO", c // 4)],
                   [("ps", pb + hc)])
        pso = psum[:, pb:pb + 2, :].rearrange("p a b -> p (a b)")
        act(junk2, pso, AF.Square, [("ps", pb), ("ps", pb + 1)], ["junk2", ("ss_e", i)], accum=ss_e[:, i:i + 1])
        act(ss_e[:, i:i + 1], ss_e[:, i:i + 1], AF.Sqrt, [("ss_e", i)], [("ss_e", i)], scale=1.0 / 1024, bias=EPS)
        dve("reciprocal", [("ss_e", i)], [("rs_e", i)], rs_e[:, i:i + 1], ss_e[:, i:i + 1])
        dve("scalar_tensor_tensor", [("ps", pb), ("ps", pb + 1), ("rs_e", i), "Gs"], [("yt", sl)], yt[sl], pso, rs_e[:, i:i + 1], Gs[:, j, :], ALU.mult, ALU.mult)
        pool("tensor_tensor", [("yt", sl), ("xr", sl)], [("yt", sl)], yt[sl], yt[sl], xr[sl], ALU.add)
        dst = y_s[i * 128:(i + 1) * 128, :] if i < 16 else y_p[(i - 16) * 128:(i - 15) * 128, :]
        dma("gpsimd", dst, yt[sl], [("yt", sl)], [], ("yt", sl))
    if STOP_AT is not None:
        P.cut = STOP_AT if STOP_AT > 10 else P.marks[STOP_AT]
    print('marks', P.marks, 'nops', len(P.ops), notes[:14])
    P.emit()
    return nc


_NC_CACHE = {}


def _rope_tables(rev):
    s = np.arange(T_S)
    t = (T_S - 1 - s) if rev else s
    row = (t // 64).astype(np.float32)
    col = (t % 64).astype(np.float32)
    inv = (np.float32(10000.0) ** (-np.arange(16, dtype=np.float32) * np.float32(2.0) / np.float32(32))).astype(np.float32)
    cos = np.zeros((128, T_S), np.float32)
    ssin = np.zeros((128, T_S), np.float32)
    for jj in range(128):
        d = jj % 64
        pos = row if d < 32 else col
        ang = (pos * inv[d % 16]).astype(np.float32)
        cos[jj] = np.cos(ang)
        sgn = -1.0 if (d % 32) < 16 else 1.0
        ssin[jj] = sgn * np.sin(ang)
    return cos, ssin


def kernel(x_prompt, x_sample, cache_k, cache_v, state_lru, c, c_ctx, w_ada, b_ada, g_pre, w_in,
           lambda_q1, lambda_k1, lambda_q2, lambda_k2, g_subln, conv_w, conv_b,
           w_rgate, b_rgate, w_igate, b_igate, lru_lambda, w_out, g_post):
    f = lambda a: np.ascontiguousarray(np.asarray(a, dtype=np.float32))
    x_prompt, x_sample, cache_k, cache_v, state_lru = map(f, (x_prompt, x_sample, cache_k, cache_v, state_lru))
    c, c_ctx, w_ada, b_ada, g_pre, w_in = map(f, (c, c_ctx, w_ada, b_ada, g_pre, w_in))
    w_out, g_post, g_subln, conv_w, conv_b = map(f, (w_out, g_post, g_subln, conv_w, conv_b))
    w_rgate, b_rgate, w_igate, b_igate, lru_lambda = map(f, (w_rgate, b_rgate, w_igate, b_igate, lru_lambda))
    lq1, lk1, lq2, lk2 = map(f, (lambda_q1, lambda_k1, lambda_q2, lambda_k2))

    if "nc" not in _NC_CACHE:
        _NC_CACHE["nc"] = build_program()
    nc = _NC_CACHE["nc"]

    W = w_in[0]
    perm = np.array([jj + 16 if (jj % 32) < 16 else jj - 16 for jj in range(128)])
    pmat = np.zeros((128, 128), np.float32)
    pmat[perm, np.arange(128)] = 1.0
    w_att = np.empty((8, 1024, 512), np.float32)
    w_lru = np.empty((8, 1024, 256), np.float32)
    for h in range(8):
        q = W[:, h * 128:(h + 1) * 128]
        k = W[:, 1024 + h * 128:1024 + (h + 1) * 128]
        v = W[:, 2048 + h * 128:2048 + (h + 1) * 128]
        g = W[:, 3072 + h * 128:3072 + (h + 1) * 128]
        w_att[h] = np.concatenate([q, k, v, g], axis=1)
        w_lru[h] = np.concatenate([W[:, 4096 + h * 128:4096 + (h + 1) * 128], W[:, 5120 + h * 128:5120 + (h + 1) * 128]], axis=1)
    col = lambda vec: np.ascontiguousarray(vec.reshape(-1, 128).T)
    rep = lambda vec: np.ascontiguousarray(np.broadcast_to(vec[None, :], (128, vec.shape[0])))
    bada_ss = col(b_ada[0, 0:2048])
    bada_g = rep(b_ada[0, 2048:3072])
    gpre_c = col(g_pre[0])
    gpost_r = rep(g_post[0])
    gsub_r = rep(g_subln[0])
    lam_v = np.ascontiguousarray(np.broadcast_to(np.concatenate([lq1[0], lq2[0], lk1[0], lk2[0]])[None, :], (128, 256)))
    ropes = {False: _rope_tables(False), True: _rope_tables(True)}

    in_maps = []
    for core in range(NCORES):
        b, half = core // 2, core % 2
        rev = half == 1
        df, db = (1, 0) if rev else (0, 1)
        xs = x_sample[b][::-1] if rev else x_sample[b]
        xpp = x_prompt[2 * core:2 * core + 2]
        if rev:
            xpp = xpp[:, ::-1]
        cs = np.stack([c[b].reshape(8, 128).T, c_ctx.reshape(8, 128).T], axis=-1)
        taps = np.zeros((5, 1024), np.float32)
        if rev:
            taps[0:4] = conv_w[0][::-1]
        else:
            taps[1:5] = conv_w[0]
        vecs = [taps[0], taps[1], taps[2], taps[3], taps[4], conv_b[0], b_rgate[0, df], b_rgate[0, db], b_igate[0, df], b_igate[0, db],
                lru_lambda[0, df], lru_lambda[0, db], state_lru[b, 0, df], state_lru[b, 0, db]]
        lru_vec = np.stack([col(vv) for vv in vecs], axis=-1)
        wg = np.stack([w_rgate[0, df], w_igate[0, df], w_rgate[0, db], w_igate[0, db]], axis=1)
        wg = np.ascontiguousarray(wg.transpose(2, 0, 1, 3)).reshape(128, 8 * 4 * 128)
        cos, ssin = ropes[rev]
        in_maps.append({
            "xs": np.ascontiguousarray(xs), "xp": np.ascontiguousarray(xpp.reshape(T_P, 1024)),
            "ck": np.ascontiguousarray(cache_k[b, 0].reshape(512, 1024)), "cv": np.ascontiguousarray(cache_v[b, 0].reshape(512, 1024)),
            "cs": np.ascontiguousarray(cs.reshape(128, 16)), "w_ada": w_ada[0], "bada_ss": bada_ss, "bada_g": bada_g,
            "gpre_c": gpre_c, "gpost_r": gpost_r, "w_att": w_att, "pmat": pmat, "rope_cos": cos, "rope_ssin": ssin, "w_lru": w_lru,
            "w_gate": wg, "lru_vec": np.ascontiguousarray(lru_vec.reshape(128, 8 * 14)), "w_out": w_out[0], "lam_v": lam_v, "gsub_r": gsub_r,
        })
    res = run_bass_kernel_spmd(nc, in_maps, core_ids=list(range(NCORES)))

    y_prompt = np.empty((16, 256, 1024), np.float32)
    y_sample = np.empty((4, 4096, 1024), np.float32)
    new_k = np.empty((16, 1, 256, 8, 128), np.float32)
    new_v = np.empty((16, 1, 256, 8, 128), np.float32)
    new_st = np.empty((16, 1, 2, 1024), np.float32)
    for core in range(NCORES):
        r = res.results[core]
        b, half = core // 2, core % 2
        rev = half == 1
        ys = r["y_s"]
        if rev:
            y_sample[b, 2048:4096] = ys[::-1]
        else:
            y_sample[b, 0:2048] = ys
        yp = r["y_p"].reshape(2, 256, 1024)
        nkk = r["nk"].reshape(2, 256, 8, 128)
        nvv = r["nv"].reshape(2, 256, 8, 128)
        if rev:
            yp, nkk, nvv = yp[:, ::-1], nkk[:, ::-1], nvv[:, ::-1]
        y_prompt[2 * core:2 * core + 2] = yp
        new_k[2 * core:2 * core + 2, 0] = nkk
        new_v[2 * core:2 * core + 2, 0] = nvv
        st = r["nst"].reshape(128, 8, 2, 2)
        for s in range(2):
            sf = st[:, :, s, 0].T.reshape(1024)
            sb = st[:, :, s, 1].T.reshape(1024)
            if rev:
                new_st[2 * core + s, 0, 0], new_st[2 * core + s, 0, 1] = sb, sf
            else:
                new_st[2 * core + s, 0, 0], new_st[2 * core + s, 0, 1] = sf, sb
    return (y_prompt, y_sample, new_k, new_v, new_st)
```
